# Optimizing a Trainium2 kernel written in Bass

```python
import jax
import jax.numpy as jnp
from jax import lax
import numpy as np

D_MODEL = 1024
BATCH = 8
SEQ = 4096
DEPTH = 1

ATT_HEADS = 8
ATT_KV_HEADS = 2
ATT_HEAD_DIM = 128
ATT_GROUP = ATT_HEADS // ATT_KV_HEADS
ROPE_THETA = 500000.0
ROPE_FRACTION = 4
IDX_HEADS = 4
IDX_DIM = 64
TOPK_MAX = 256
Q_BLOCK = 128
SSM_EXPAND = 2
SSM_D_INNER = SSM_EXPAND * D_MODEL
SSM_HEAD_DIM = 64
SSM_HEADS = SSM_D_INNER // SSM_HEAD_DIM
SSM_GROUPS = 4
SSM_STATE = 128
SSM_CONV = 4
SSM_CHUNK = 128
N_BRANCHES = 2
EPS = 1e-6

ATT_Q_DIM = ATT_HEADS * ATT_HEAD_DIM
ATT_KV_DIM = ATT_KV_HEADS * ATT_HEAD_DIM
ATT_ROT_DIM = ATT_HEAD_DIM // ROPE_FRACTION
IDX_ROT_DIM = IDX_DIM // ROPE_FRACTION
SSM_CONV_DIM = SSM_D_INNER + 2 * SSM_GROUPS * SSM_STATE
IN_SPLITS = (ATT_Q_DIM, ATT_KV_DIM, ATT_KV_DIM, ATT_Q_DIM, IDX_HEADS * IDX_DIM, IDX_DIM, IDX_HEADS,
             SSM_D_INNER, SSM_CONV_DIM, SSM_HEADS, N_BRANCHES * D_MODEL)
IN_DIM = sum(IN_SPLITS)

kernel_name = "hybrid_dsa_mamba2_gated_block"


def _split_points():
    pts, acc = [], 0
    for s in IN_SPLITS[:-1]:
        acc += s
        pts.append(acc)
    return pts


def rms_norm(x, g):
    xf = x.astype(jnp.float32)
    y = xf * lax.rsqrt(jnp.mean(xf * xf, axis=-1, keepdims=True) + EPS)
    return y.astype(x.dtype) * g


def layer_norm(x, g, b):
    xf = x.astype(jnp.float32)
    mu = jnp.mean(xf, axis=-1, keepdims=True)
    var = jnp.mean(jnp.square(xf - mu), axis=-1, keepdims=True)
    return ((xf - mu) * lax.rsqrt(var + EPS)).astype(x.dtype) * g + b


def rope_tables(positions, rot_dim):
    inv = jnp.power(ROPE_THETA, -(jnp.arange(0, rot_dim, 2, dtype=jnp.float32) / rot_dim))
    ang = positions.astype(jnp.float32)[..., None] * inv
    return jnp.cos(ang)[:, :, None, :], jnp.sin(ang)[:, :, None, :]


def partial_rope(x, cos, sin):
    half = cos.shape[-1]
    rot = 2 * half
    x1 = x[..., :half].astype(jnp.float32)
    x2 = x[..., half:rot].astype(jnp.float32)
    r1 = x1 * cos - x2 * sin
    r2 = x2 * cos + x1 * sin
    return jnp.concatenate([r1.astype(x.dtype), r2.astype(x.dtype), x[..., rot:]], axis=-1)


def causal_depthwise_conv(u, w, b):
    ch = u.shape[-1]
    y = lax.conv_general_dilated(u, w[:, None, :].astype(u.dtype), window_strides=(1,),
                                 padding=[(SSM_CONV - 1, 0)],
                                 dimension_numbers=("NWC", "WIO", "NWC"),
                                 feature_group_count=ch)
    return y + b


def segsum(a):
    n = a.shape[-1]
    ar = jnp.broadcast_to(a[..., :, None], a.shape + (n,))
    ar = jnp.where(jnp.tril(jnp.ones((n, n), dtype=bool), -1), ar, 0.0)
    cs = jnp.cumsum(ar, axis=-2)
    return jnp.where(jnp.tril(jnp.ones((n, n), dtype=bool)), cs, -jnp.inf)


def ssd(xh, dt, a, bm, cm):
    bsz, s, h, p = xh.shape
    g, n = bm.shape[2], bm.shape[3]
    r = h // g
    nc, ln = s // SSM_CHUNK, SSM_CHUNK
    xc = (xh * dt[..., None]).reshape(bsz, nc, ln, g, r, p)
    adt = (dt * a).reshape(bsz, nc, ln, g, r).transpose(0, 1, 3, 4, 2)
    a_cum = jnp.cumsum(adt, axis=-1)
    bc = bm.reshape(bsz, nc, ln, g, n)
    cc = cm.reshape(bsz, nc, ln, g, n)
    decay = jnp.exp(segsum(adt))
    cb = jnp.einsum("bclgn,bcsgn->bcgls", cc, bc)
    y_diag = jnp.einsum("bcgls,bcgrls,bcsgrp->bclgrp", cb, decay, xc)
    decay_states = jnp.exp(a_cum[..., -1:] - a_cum)
    states = jnp.einsum("bclgn,bcgrl,bclgrp->bcgrpn", bc, decay_states, xc)
    chunk_decay = jnp.exp(a_cum[..., -1])

    def step(hstate, inp):
        st, dec = inp
        return dec[..., None, None] * hstate + st, hstate

    h0 = jnp.zeros((bsz, g, r, p, n), dtype=xc.dtype)
    _, prev = lax.scan(step, h0, (states.swapaxes(0, 1), chunk_decay.swapaxes(0, 1)))
    prev = prev.swapaxes(0, 1)
    y_off = jnp.einsum("bclgn,bcgrpn,bcgrl->bclgrp", cc, prev, jnp.exp(a_cum))
    return (y_diag + y_off).reshape(bsz, s, h, p)


def sparse_attention(q, k, v, q_idx, k_idx, w_idx):
    bsz, s = q.shape[0], q.shape[1]
    topk = min(TOPK_MAX, s // 4)
    nblk = s // Q_BLOCK
    scale = ATT_HEAD_DIM ** -0.5
    key_pos = jnp.arange(s)
    gather = jax.vmap(lambda arr, idx: arr[idx])

    def to_blocks(arr):
        return arr.reshape((bsz, nblk, Q_BLOCK) + arr.shape[2:]).swapaxes(0, 1)

    qg = q.reshape(bsz, s, ATT_KV_HEADS, ATT_GROUP, ATT_HEAD_DIM)

    def block(args):
        blk, qb, qib, wb = args
        qpos = blk * Q_BLOCK + jnp.arange(Q_BLOCK)
        causal = key_pos[None, :] <= qpos[:, None]
        logits = jnp.einsum("bthd,bsd->bths", qib, k_idx)
        score = jnp.einsum("bth,bths->bts", wb, jax.nn.relu(logits))
        score = jnp.where(causal[None], score, -jnp.inf)
        _, sel = lax.top_k(score, topk)
        valid = sel <= qpos[None, :, None]
        ks = gather(k, sel)
        vs = gather(v, sel)
        sc = jnp.einsum("btgrd,btjgd->btgrj", qb, ks).astype(jnp.float32) * scale
        sc = jnp.where(valid[:, :, None, None, :], sc, -jnp.inf)
        pr = jax.nn.softmax(sc, axis=-1).astype(vs.dtype)
        return jnp.einsum("btgrj,btjgd->btgrd", pr, vs)

    out = lax.map(block, (jnp.arange(nblk), to_blocks(qg), to_blocks(q_idx), to_blocks(w_idx)))
    return out.swapaxes(0, 1).reshape(bsz, s, ATT_Q_DIM)


def setup_inputs(seed: int = 0) -> dict:
    key = jax.random.key(seed)
    ks = jax.random.split(key, 24)
    nrm = jax.random.normal
    f32 = jnp.float32
    dt0 = jnp.exp(jax.random.uniform(ks[14], (DEPTH, SSM_HEADS), dtype=f32)
                  * (np.log(0.1) - np.log(0.001)) + np.log(0.001))
    return {
        "x": nrm(ks[0], (BATCH, SEQ, D_MODEL), f32),
        "c": nrm(ks[1], (BATCH, D_MODEL), f32),
        "positions": jnp.broadcast_to(jnp.arange(SEQ, dtype=jnp.int32)[None, :], (BATCH, SEQ)),
        "ada_w": nrm(ks[2], (DEPTH, D_MODEL, 3 * D_MODEL), f32) * (0.5 * D_MODEL ** -0.5),
        "ada_b": nrm(ks[3], (DEPTH, 3 * D_MODEL), f32) * 0.01,
        "norm_g": 1.0 + 0.1 * nrm(ks[4], (DEPTH, D_MODEL), f32),
        "w_in": nrm(ks[5], (DEPTH, D_MODEL, IN_DIM), f32) * D_MODEL ** -0.5,
        "q_norm_g": 1.0 + 0.1 * nrm(ks[6], (DEPTH, ATT_HEAD_DIM), f32),
        "k_norm_g": 1.0 + 0.1 * nrm(ks[7], (DEPTH, ATT_HEAD_DIM), f32),
        "idx_k_ln_g": 1.0 + 0.1 * nrm(ks[8], (DEPTH, IDX_DIM), f32),
        "idx_k_ln_b": 0.01 * nrm(ks[9], (DEPTH, IDX_DIM), f32),
        "conv_w": nrm(ks[10], (DEPTH, SSM_CONV, SSM_CONV_DIM), f32) * SSM_CONV ** -0.5,
        "conv_b": 0.01 * nrm(ks[11], (DEPTH, SSM_CONV_DIM), f32),
        "dt_bias": dt0 + jnp.log(-jnp.expm1(-dt0)),
        "a_log": jnp.log(jax.random.uniform(ks[12], (DEPTH, SSM_HEADS), f32, 1.0, 16.0)),
        "d_skip": 1.0 + 0.1 * nrm(ks[13], (DEPTH, SSM_HEADS), f32),
        "ssm_norm_g": 1.0 + 0.1 * nrm(ks[15], (DEPTH, SSM_D_INNER), f32),
        "w_branch_att": nrm(ks[16], (DEPTH, ATT_Q_DIM, D_MODEL), f32) * ATT_Q_DIM ** -0.5,
        "w_branch_ssm": nrm(ks[17], (DEPTH, SSM_D_INNER, D_MODEL), f32) * SSM_D_INNER ** -0.5,
        "w_out": nrm(ks[18], (DEPTH, D_MODEL, D_MODEL), f32) * D_MODEL ** -0.5,
    }


def reference(x, c, positions, ada_w, ada_b, norm_g, w_in, q_norm_g, k_norm_g, idx_k_ln_g, idx_k_ln_b,
              conv_w, conv_b, dt_bias, a_log, d_skip, ssm_norm_g, w_branch_att, w_branch_ssm, w_out):
    f32 = jnp.float32
    bsz, s, _ = x.shape
    cos_a, sin_a = rope_tables(positions, ATT_ROT_DIM)
    cos_i, sin_i = rope_tables(positions, IDX_ROT_DIM)
    pts = _split_points()
    for l in range(DEPTH):
        mod = jax.nn.silu(c) @ ada_w[l] + ada_b[l]
        shift, scale, gate = jnp.split(mod, 3, axis=-1)
        h = rms_norm(x, norm_g[l]) * (1.0 + scale[:, None, :]) + shift[:, None, :]
        proj = h @ w_in[l]
        (q, k, v, z_att, q_idx, k_idx, w_idx, z_ssm, xbc, dt_raw, gate_logits) = jnp.split(proj, pts, axis=-1)

        q = partial_rope(rms_norm(q.reshape(bsz, s, ATT_HEADS, ATT_HEAD_DIM), q_norm_g[l]), cos_a, sin_a)
        k = partial_rope(rms_norm(k.reshape(bsz, s, ATT_KV_HEADS, ATT_HEAD_DIM), k_norm_g[l]), cos_a, sin_a)
        v = v.reshape(bsz, s, ATT_KV_HEADS, ATT_HEAD_DIM)
        q_idx = partial_rope(q_idx.reshape(bsz, s, IDX_HEADS, IDX_DIM), cos_i, sin_i) * (IDX_DIM ** -0.5)
        k_idx = partial_rope(layer_norm(k_idx, idx_k_ln_g[l], idx_k_ln_b[l])[:, :, None, :], cos_i, sin_i)[:, :, 0, :]
        w_idx = w_idx * (IDX_HEADS ** -0.5)
        o_att = sparse_attention(q, k, v, q_idx, k_idx, w_idx) * jax.nn.silu(z_att)
        y_att = o_att @ w_branch_att[l]

        xbc = jax.nn.silu(causal_depthwise_conv(xbc, conv_w[l], conv_b[l]))
        xs, bm, cm = jnp.split(xbc, [SSM_D_INNER, SSM_D_INNER + SSM_GROUPS * SSM_STATE], axis=-1)
        xs = xs.reshape(bsz, s, SSM_HEADS, SSM_HEAD_DIM).astype(f32)
        dt = jax.nn.softplus(dt_raw.astype(f32) + dt_bias[l].astype(f32))
        a = -jnp.exp(a_log[l].astype(f32))
        y = ssd(xs, dt, a,
                bm.reshape(bsz, s, SSM_GROUPS, SSM_STATE).astype(f32),
                cm.reshape(bsz, s, SSM_GROUPS, SSM_STATE).astype(f32))
        y = y + d_skip[l].astype(f32)[:, None] * xs
        y = y.reshape(bsz, s, SSM_D_INNER).astype(x.dtype) * jax.nn.silu(z_ssm)
        y = rms_norm(y.reshape(bsz, s, SSM_GROUPS, SSM_D_INNER // SSM_GROUPS),
                     ssm_norm_g[l].reshape(SSM_GROUPS, SSM_D_INNER // SSM_GROUPS)).reshape(bsz, s, SSM_D_INNER)
        y_ssm = y @ w_branch_ssm[l]

        g_att, g_ssm = jnp.split(gate_logits, 2, axis=-1)
        merged = jax.nn.sigmoid(g_att) * y_att + jax.nn.sigmoid(g_ssm) * y_ssm
        x = x + gate[:, None, :] * (merged @ w_out[l])
    return x
```

```python
import numpy as np
import ml_dtypes
import concourse.bass as bass
import concourse.mybir as mybir
from concourse.bass_utils import run_bass_kernel_spmd

F32 = mybir.dt.float32
BF16 = mybir.dt.bfloat16
I32 = mybir.dt.int32
AF = mybir.ActivationFunctionType
ALU = mybir.AluOpType

SEQ = 4096
D = 1024
NT = SEQ // 128
NSUB = 2
NSUP = NT // NSUB
TS = NSUB * 128
IN_DIM = 10084
NITER = 16
BIS1_DVE_FRAC = 0.6
BIS_DVE_FRAC = 0.5
TOPK = 256
NEG = -30000.0
BIGNEG = -1.0e30

PE, ACT, DVE, POOL, SP = 0, 1, 2, 3, 4
NDMA = 24


class Op:
    __slots__ = ("eng", "idx", "emit", "waits", "flag", "clk", "dma", "val")

    def __init__(self, eng, idx, emit):
        self.eng = eng
        self.idx = idx
        self.emit = emit
        self.waits = []
        self.flag = False
        self.clk = None
        self.dma = None
        self.val = 0


class Sched:
    def __init__(self):
        self.ops = [[] for _ in range(5)]
        self.clk = [[-1] * 5 for _ in range(5)]
        self.dclk = [dict() for _ in range(5)]
        self.reg = {}
        self.retired = []
        self.dma_rr = 0
        self.dma_last = [None] * NDMA
        self.dma_uses = [0] * NDMA

    def _need(self, op, p):
        e = op.eng
        clk = self.clk[e]
        if p.dma is None:
            if p.eng == PE and e == PE:
                return
            if clk[p.eng] >= p.idx:
                return
        else:
            k, u = p.dma
            if self.dclk[e].get(k, 0) >= u:
                return
            self.dclk[e][k] = u
        p.flag = True
        op.waits.append(p)
        pc = p.clk
        for i in range(5):
            if pc[i] > clk[i]:
                clk[i] = pc[i]

    def _state(self, key):
        st = self.reg.get(key)
        if st is None:
            if key.startswith("A:"):
                st = [None, {}, list(self.retired)]
            else:
                st = [None, {}, []]
            self.reg[key] = st
        return st

    def add(self, eng, emit, rd=(), wr=(), dma=False):
        op = Op(eng, len(self.ops[eng]), emit)
        deps = []
        for r in rd:
            st = self._state(r)
            if st[0] is not None:
                deps.append(st[0])
        for w in wr:
            st = self._state(w)
            if st[0] is not None:
                deps.append(st[0])
            deps.extend(st[1].values())
            deps.extend(st[2])
        for p in deps:
            self._need(op, p)
        if dma:
            k = self.dma_rr
            self.dma_rr = (k + 1) % NDMA
            prev = self.dma_last[k]
            if prev is not None:
                self._need(op, prev)
            self.dma_uses[k] += 1
            op.dma = (k, self.dma_uses[k])
            op.flag = True
            self.dma_last[k] = op
            op.clk = tuple(self.clk[eng])
        else:
            c = list(self.clk[eng])
            c[eng] = op.idx
            op.clk = tuple(c)
        for r in rd:
            st = self.reg[r]
            if dma:
                st[2].append(op)
            else:
                st[1][eng] = op
        for w in wr:
            st = self.reg[w]
            st[0] = op
            st[1] = {}
            st[2] = []
        self.ops[eng].append(op)
        return op

    def retire_arena(self):
        ops = list(self.retired)
        for key in [k for k in self.reg if k.startswith("A:")]:
            st = self.reg.pop(key)
            if st[0] is not None:
                ops.append(st[0])
            ops.extend(st[1].values())
            ops.extend(st[2])
        best = {}
        dm = []
        for p in ops:
            if p.dma is not None:
                dm.append(p)
            elif p.eng not in best or best[p.eng].idx < p.idx:
                best[p.eng] = p
        self.retired = list(best.values()) + dm[-64:]

    def emit_all(self, nc, block, esem, dsem):
        for e in range(5):
            cnt = 0
            for op in self.ops[e]:
                if op.flag and op.dma is None:
                    cnt += 1
                    op.val = cnt

        def run(e, h):
            for op in self.ops[e]:
                best = {}
                for p in op.waits:
                    if p.dma is None:
                        key = ("e", p.eng)
                        v = p.val
                    else:
                        key = ("d", p.dma[0])
                        v = 16 * p.dma[1]
                    if best.get(key, -1) < v:
                        best[key] = v
                for (kind, i), v in best.items():
                    h.wait_ge(esem[i] if kind == "e" else dsem[i], v)
                if op.emit is None:
                    continue
                ins = op.emit(h)
                if op.flag:
                    if op.dma is None:
                        ins.then_inc(esem[e], 1)
                    else:
                        ins.then_inc(dsem[op.dma[0]], 16)

        block.tensor(lambda h: run(PE, h))
        block.scalar(lambda h: run(ACT, h))
        block.vector(lambda h: run(DVE, h))
        block.gpsimd(lambda h: run(POOL, h))
        block.sync(lambda h: run(SP, h))


def _cw_consts():
    two_pi = 2.0 * np.pi
    c1 = np.float32(6.28125)
    r = two_pi - float(c1)
    c2 = np.float32(r)
    b = c2.view(np.uint32) & np.uint32(0xFFFFF000)
    c2 = np.array(b, dtype=np.uint32).view(np.float32)
    c3 = np.float32(r - float(c2))
    return float(c1), float(c2), float(c3)


def _chunks():
    ch = []
    def w_in(name, c0, w):
        ch.append((name, "w_in", 0, c0, w))
    w_in("kv", 1024, 512)
    w_in("idx", 2560, 324)
    w_in("q0", 0, 512)
    w_in("q1", 512, 512)
    w_in("z0", 1536, 512)
    w_in("z1", 2048, 512)
    w_in("dt", 8004, 32)
    for g in range(4):
        w_in("zs%d" % g, 2884 + 512 * g, 512)
        w_in("xs%d" % g, 4932 + 512 * g, 512)
    for g in range(4):
        ch.append(("bc%d" % g, "w_in", 0, [6980 + 128 * g, 7492 + 128 * g], 128))
    w_in("ga0", 8036, 512)
    w_in("ga1", 8548, 512)
    w_in("gs0", 9060, 512)
    w_in("gs1", 9572, 512)
    for c in range(2):
        ch.append(("ba%d" % c, "w_ba", 0, 512 * c, 512))
    for kh in range(2):
        for c in range(2):
            ch.append(("bs%d%d" % (kh, c), "w_bs", 1024 * kh, 512 * c, 512))
    for c in range(2):
        ch.append(("wo%d" % c, "w_out", 0, 512 * c, 512))
    return ch


CHUNKS = _chunks()
CHIDX = {c[0]: i for i, c in enumerate(CHUNKS)}
CHW = {c[0]: (c[4] * len(c[3]) if isinstance(c[3], list) else c[4]) for c in CHUNKS}


class _Stop(Exception):
    pass


def build_nc(debug=None, stop=None):
    nc = bass.Bass("TRN2", target_bir_lowering=False)
    S = Sched()

    def mark(n):
        if stop is not None and stop == n:
            raise _Stop()

    def din(name, shape, dt=F32):
        return nc.dram_tensor(name, shape, dt, kind="ExternalInput").ap()

    x_d = din("x", [SEQ, D])
    c_d = din("c2", [128, 8])
    pos_d = din("pos2", [128, NT], I32)
    adaw_d = din("ada_w", [D, 3 * D])
    pk1_d = din("pk1", [1, 4096])
    pk2_d = din("pk2", [1, 512])
    convp_d = din("convp", [128, 24, 5])
    gs16_d = din("gs16", [128, 16])
    wsrc = {
        "w_in": din("w_in", [D, IN_DIM]),
        "w_ba": din("w_ba", [D, D]),
        "w_bs": din("w_bs", [2 * D, D]),
        "w_out": din("w_out", [D, D]),
    }
    out_d = nc.dram_tensor("out", [SEQ, D], F32, kind="ExternalOutput").ap()
    wsc_d = nc.dram_tensor("wscratch", [len(CHUNKS), 128, 8 * 512], BF16, kind="Internal").ap()
    dbg_d = {}

    from contextlib import ExitStack
    es = ExitStack()
    with es:
        def sb(name, shape, dt=F32):
            return es.enter_context(nc.sbuf_tensor(name, shape, dt))

        def ps(name, shape, dt=F32):
            return es.enter_context(nc.psum_tensor(name, shape, dt))

        esem = [es.enter_context(nc.semaphore("es%d" % i)) for i in range(5)]
        dsem = [es.enter_context(nc.semaphore("ds%d" % i)) for i in range(NDMA)]

        ident_f = sb("ident_f", [128, 128])
        ident_b = sb("ident_b", [128, 128], BF16)
        U_f = sb("U_f", [128, 128])
        G_f = sb("G_f", [128, 128])
        ones_f = sb("ones_f", [128, 128])
        mhalf = sb("mhalf", [128, 8])
        I4_b = sb("I4_b", [128, 512], BF16)
        modbc = sb("modbc", [128, 3 * D])
        pk2 = sb("pk2s", [128, 512])
        abc = sb("abc", [128, 32])
        convp = sb("convps", [128, 24, 5])
        gs16 = sb("gs16s", [128, 16])
        cosA = sb("cosA", [128, NT, 16])
        sinA = sb("sinA", [128, NT, 16])
        cosI = sb("cosI", [128, NT, 8])
        sinI = sb("sinI", [128, NT, 8])
        hT = sb("hT", [128, 8, TS], BF16)
        KT = sb("KT", [128, 2, SEQ], BF16)
        Vaug = sb("Vaug", [128, NT, 2, 130], BF16)
        kidxT = sb("kidxT", [64, SEQ], BF16)
        St = sb("St", [128, 4, 512])
        Stb = sb("Stb", [128, 4, 512], BF16)
        ctail = sb("ctail", [128, 24, 3])
        NRING = 4
        ring = [sb("ring%d" % i, [128, 8, 512], BF16) for i in range(NRING)]
        yatt = sb("yatt", [128, NSUB, D])
        dtw = sb("dtw", [128, 8, 32], BF16)
        ynT = sb("ynT", [128, 16, TS], BF16)
        ARENA_F = 20192
        arena = sb("arena", [128, ARENA_F])

        banks = [ps("bank%d" % i, [128, 512]) for i in range(8)]

        class Carver:
            def __init__(self):
                self.off = 0

            def reset(self):
                self.off = 0
                S.retire_arena()

            def f32(self, n):
                a = arena[:, self.off:self.off + n]
                self.off += n
                assert self.off <= ARENA_F, self.off
                return a

            def bf16(self, n):
                w = (n + 1) // 2
                a = arena[:, self.off:self.off + w].bitcast(BF16)
                self.off += w
                assert self.off <= ARENA_F, self.off
                return a

        CV = Carver()

        gen_pool = [0, 1]
        gen_i = [0]

        def gen_bank():
            b = gen_pool[gen_i[0] % len(gen_pool)]
            gen_i[0] += 1
            return b

        def bk(i):
            return "bank%d" % i

        rr = [0]

        def dma(eng, out, in_, rd, wr):
            h = {SP: None}
            return S.add(eng, lambda e: e.dma_start(out=out, in_=in_), rd=rd, wr=wr, dma=True)

        def mm(out, lhsT, rhs, start, stop, rd, wr):
            return S.add(PE, lambda e: e.matmul(out, lhsT, rhs, start=start, stop=stop), rd=rd, wr=wr)

        def tr(out, in_, ident, rd, wr):
            return S.add(PE, lambda e: e.transpose(out, in_, ident), rd=rd, wr=wr)

        def act(out, in_, func, rd, wr, bias=None, scale=None, accum=None):
            kw = {}
            if bias is not None:
                kw["bias"] = bias
            if scale is not None:
                kw["scale"] = scale
            if accum is not None:
                kw["accum_out"] = accum
            return S.add(ACT, lambda e: e.activation(out, in_, func, **kw), rd=rd, wr=wr)

        def tsc(eng, out, in0, s1, s2, op0, op1, rd, wr, accum=None):
            kw = {}
            if accum is not None:
                kw["accum_out"] = accum
            if op1 is None:
                return S.add(eng, lambda e: e.tensor_scalar(out, in0, s1, None, op0, **kw), rd=rd, wr=wr)
            return S.add(eng, lambda e: e.tensor_scalar(out, in0, s1, s2, op0, op1, **kw), rd=rd, wr=wr)

        def stt(out, in0, sc, in1, op0, op1, rd, wr):
            return S.add(DVE, lambda e: e.scalar_tensor_tensor(out, in0, sc, in1, op0, op1), rd=rd, wr=wr)

        def tt(eng, out, in0, in1, op, rd, wr):
            return S.add(eng, lambda e: e.tensor_tensor(out, in0, in1, op), rd=rd, wr=wr)

        def cp(eng, out, in_, rd, wr, psum=False):
            if eng == DVE and psum:
                return S.add(DVE, lambda e: e.tensor_scalar(out, in_, 1.0, None, ALU.mult), rd=rd, wr=wr)
            if eng == ACT:
                return S.add(ACT, lambda e: e.copy(out, in_), rd=rd, wr=wr)
            return S.add(eng, lambda e: e.tensor_copy(out, in_), rd=rd, wr=wr)

        def memset(eng, ap, val, wr):
            return S.add(eng, lambda e: e.memset(ap, val), rd=(), wr=wr)

        def recip(out, in_, rd, wr):
            return S.add(DVE, lambda e: e.reciprocal(out, in_), rd=rd, wr=wr)

        def rstd_pool(out, tmp, in_, n, scale, rd, wr, tkey):
            tsc(POOL, tmp, in_, scale, 1e-6, ALU.mult, ALU.add, rd, [tkey])
            tt(POOL, out, tmp, mhalf[:, 0:n], ALU.pow, [tkey, "mhalf"], wr)

        def dbg(name, ap_sb, rd, dt=F32):
            if not debug or name in dbg_d:
                return
            shp = list(ap_sb.shape)
            t = nc.dram_tensor("dbg_" + name, shp, dt, kind="ExternalOutput").ap()
            dbg_d[name] = t
            dma(SP, t, ap_sb, rd=rd, wr=["dbg_" + name])

        try:
            memset(POOL, ones_f[:], 1.0, ["ones_f"])
            memset(POOL, mhalf[:], -0.5, ["mhalf"])
            S.add(POOL, lambda e: e.affine_select(ident_f[:], ones_f[:], [[-1, 128]], ALU.is_equal, 0.0,
                                                  base=0, channel_multiplier=1), rd=["ones_f"], wr=["ident_f"])
            S.add(POOL, lambda e: e.affine_select(U_f[:], ones_f[:], [[1, 128]], ALU.is_ge, 0.0,
                                                  base=0, channel_multiplier=-1), rd=["ones_f"], wr=["U_f"])
            S.add(POOL, lambda e: e.affine_select(G_f[:], ones_f[:], [[-1, 128]], ALU.is_gt, 0.0,
                                                  base=0, channel_multiplier=1), rd=["ones_f"], wr=["G_f"])
            cp(POOL, ident_b[:], ident_f[:], ["ident_f"], ["ident_b"])
            for i in range(4):
                cp(POOL, I4_b[:, 128 * i:128 * (i + 1)], ident_f[:], ["ident_f"], ["I4_b"])
            memset(POOL, St[:], 0.0, ["St%d" % g for g in range(4)])
            memset(POOL, Stb[:], 0.0, ["Stb%d" % g for g in range(4)])
            memset(POOL, ctail[:], 0.0, ["ctail%d" % c for c in range(24)])
            memset(POOL, Vaug[:, :, :, 128:130], 1.0, ["Vones"])

            mark(1)
            CV.off = 0
            wbf_base = CV.off
            pk1 = CV.f32(4096)
            c2s = CV.f32(8)
            scs = CV.f32(8)
            scb = CV.f32(8 * 128).rearrange("p (k m) -> p k m", m=128)
            posi = CV.f32(NT).bitcast(I32)
            posf = CV.f32(NT)
            inv = CV.f32(16)
            ang = CV.f32(NT * 16)
            nn_ = CV.f32(NT * 16)
            rr_ = CV.f32(NT * 16)
            aa_ = CV.f32(NT * 16)
            adaw = [CV.f32(8 * 512).rearrange("p (k c) -> p k c", c=512) for _ in range(2)]
            wst = adaw
            wbf = [arena[:, wbf_base + 2048 * i:wbf_base + 2048 * (i + 1)].bitcast(BF16).rearrange("p (k c) -> p k c", c=512)
                   for i in range(2)]
            PK1 = ["A:wbf0", "A:wbf1"]

            dma(SP, pk1, pk1_d.partition_broadcast(128), [], PK1)
            dma(SP, pk2[:], pk2_d.partition_broadcast(128), [], ["pk2"])
            dma(SP, c2s, c_d, [], ["A:c2s"])
            dma(SP, posi, pos_d, [], ["A:posi"])
            dma(SP, convp[:], convp_d, [], ["convp"])
            dma(SP, gs16[:], gs16_d, [], ["gs16"])
            QG, KG, IG, IB, DTB, ALOG, DSK = 0, 128, 256, 320, 384, 416, 448

            act(abc[:], pk2[:, ALOG:ALOG + 32], AF.Exp, ["pk2"], ["abc"])
            tsc(DVE, abc[:], abc[:], -1.0, None, ALU.mult, None, ["abc"], ["abc"])

            mark(2)
            act(scs, c2s, AF.Silu, ["A:c2s"], ["A:scs"])
            cp(DVE, scb, scs.unsqueeze(2).to_broadcast([128, 8, 128]), ["A:scs"], ["A:scb"])
            adaw_v = adaw_d.rearrange("(k p) c -> p k c", p=128)
            for n in range(6):
                buf = adaw[n % 2]
                key = "A:adaw%d" % (n % 2)
                dma(SP, buf, adaw_v[:, :, 512 * n:512 * (n + 1)], [], [key])
                b = n % 2
                for k in range(8):
                    mm(banks[b][:], scb[:, k, :], buf[:, k, :], k == 0, k == 7, ["A:scb", key], [bk(b)])
                tt(DVE, modbc[:, 512 * n:512 * (n + 1)], banks[b][:], pk1[:, 512 * n:512 * (n + 1)], ALU.add,
                   [bk(b)] + PK1, ["modbc"])
            stt(modbc[:, D:2 * D], modbc[:, D:2 * D], 1.0, pk1[:, 3072:4096], ALU.add, ALU.mult,
                ["modbc"] + PK1, ["modbc"])

            dbg("modbc", modbc[:], ["modbc"])
            dbg("abc", abc[:], ["abc"])
            mark(3)
            c1, c2c, c3 = _cw_consts()
            MAGIC = 12582912.0
            PI_IN = 3.1415925
            cp(DVE, posf, posi, ["A:posi"], ["A:posf"])

            def rope_table(rot, cos_t, sin_t, tag):
                half = rot // 2
                for i in range(half):
                    v = float(np.float32(500000.0) ** np.float32(-(2.0 * i) / rot))
                    memset(POOL, inv[:, i:i + 1], v, ["A:inv"])
                n = NT * half
                a3 = ang[:, 0:n].rearrange("p (j f) -> p j f", f=half)
                tt(DVE, a3, posf.unsqueeze(2).to_broadcast([128, NT, half]),
                   inv[:, 0:half].unsqueeze(1).to_broadcast([128, NT, half]), ALU.mult,
                   ["A:posf", "A:inv"], ["A:ang"])
                a = ang[:, 0:n]
                q = nn_[:, 0:n]
                r = rr_[:, 0:n]
                ab = aa_[:, 0:n]
                tsc(DVE, q, a, float(1.0 / (2.0 * np.pi)), MAGIC, ALU.mult, ALU.add, ["A:ang"], ["A:nn"])
                tsc(DVE, q, q, MAGIC, None, ALU.subtract, None, ["A:nn"], ["A:nn"])
                stt(r, q, -c1, a, ALU.mult, ALU.add, ["A:nn", "A:ang"], ["A:rr"])
                stt(r, q, -c2c, r, ALU.mult, ALU.add, ["A:nn", "A:rr"], ["A:rr"])
                stt(r, q, -c3, r, ALU.mult, ALU.add, ["A:nn", "A:rr"], ["A:rr"])
                tsc(DVE, r, r, PI_IN, -PI_IN, ALU.min, ALU.max, ["A:rr"], ["A:rr"])
                act(sin_t[:].rearrange("p j f -> p (j f)"), r, AF.Sin, ["A:rr"], [tag + "sin"])
                act(ab, r, AF.Abs, ["A:rr"], ["A:aa"])
                tsc(DVE, ab, ab, -1.0, float(np.pi / 2), ALU.mult, ALU.add, ["A:aa"], ["A:aa"])
                act(cos_t[:].rearrange("p j f -> p (j f)"), ab, AF.Sin, ["A:aa"], [tag + "cos"])

            rope_table(32, cosA, sinA, "ra")
            rope_table(16, cosI, sinI, "ri")

            dbg("cosA", cosA[:], ["racos"])
            dbg("sinA", sinA[:], ["rasin"])
            dbg("cosI", cosI[:], ["ricos"])
            mark(4)
            cast_eng = [DVE, ACT]
            for ci, (name, src, r0, c0, w) in enumerate(CHUNKS):
                b = ci % 2
                skey = "A:adaw%d" % b
                okey = "A:wbf%d" % b
                if isinstance(c0, list):
                    for pi, cc0 in enumerate(c0):
                        srcv = wsrc[src][r0:r0 + 1024, cc0:cc0 + w].rearrange("(k p) c -> p k c", p=128)
                        dma(SP, wst[b][:, :, pi * w:(pi + 1) * w], srcv, [], [skey])
                    w = w * len(c0)
                else:
                    srcv = wsrc[src][r0:r0 + 1024, c0:c0 + w].rearrange("(k p) c -> p k c", p=128)
                    dma(SP, wst[b][:, :, 0:w], srcv, [], [skey])
                stg = wst[b][:, :, 0:w]
                obf = wbf[b][:, :, 0:w]
                if src == "w_bs":
                    for k in range(8):
                        kk = (r0 // 128) + k
                        tsc(DVE, obf[:, k, :], stg[:, k, :], gs16[:, kk:kk + 1], None,
                            ALU.mult, None, [skey, "gs16"], [okey])
                else:
                    e = cast_eng[ci % 2]
                    for half in range(2):
                        cp(e, obf[:, 4 * half:4 * half + 4, :], stg[:, 4 * half:4 * half + 4, :], [skey], [okey])
                dstv = wsc_d[ci].rearrange("p (k c) -> p k c", c=512)[:, :, 0:w]
                dma(ACT, dstv, obf, [okey], ["wsc%d" % ci])

            _cdt = CHIDX["dt"]
            dma(SP, dtw[:], wsc_d[_cdt].rearrange("p (k c) -> p k c", c=512)[:, :, 0:32], ["wsc%d" % _cdt], ["dtw"])
            mark(5)
            ring_i = [0]

            def load_w(name):
                i = ring_i[0] % NRING
                ring_i[0] += 1
                ci = CHIDX[name]
                w = CHW[name]
                key = "ring%d" % i
                src = wsc_d[ci].rearrange("p (k c) -> p k c", c=512)[:, :, 0:w]
                dma(SP, ring[i][:, :, 0:w], src, ["wsc%d" % ci], [key])
                return ring[i], key

            evac_i = [0]

            def evac_eng():
                evac_i[0] += 1
                return ACT if evac_i[0] % 2 == 0 else DVE

            def proj_tok(wt, wkey, u, ncols, c_off=0):
                b = gen_bank()
                for k in range(8):
                    mm(banks[b][:, 0:ncols], hT[:, k, 128 * u:128 * (u + 1)], wt[:, k, c_off:c_off + ncols],
                       k == 0, k == 7, ["hT%d" % u, wkey], [bk(b)])
                return b

            def rope_apply(eng, dst, src, nh, hd, half, cos_t, sin_t, j, tmpa, tmpb, rd, wr, tkey, copy_eng=None):
                s3 = src.rearrange("p (h d) -> p h d", d=hd)
                d3 = dst.rearrange("p (h d) -> p h d", d=hd)
                x1 = s3[:, :, 0:half]
                x2 = s3[:, :, half:2 * half]
                cb = cos_t[:, j, :].unsqueeze(1).to_broadcast([128, nh, half])
                sbb = sin_t[:, j, :].unsqueeze(1).to_broadcast([128, nh, half])
                ta = tmpa[:, 0:nh * half].rearrange("p (h d) -> p h d", d=half)
                tb = tmpb[:, 0:nh * half].rearrange("p (h d) -> p h d", d=half)
                tt(eng, ta, x1, cb, ALU.mult, rd, [tkey + "a"])
                tt(eng, tb, x2, sbb, ALU.mult, rd, [tkey + "b"])
                tt(eng, d3[:, :, 0:half], ta, tb, ALU.subtract, [tkey + "a", tkey + "b"], wr)
                tt(eng, ta, x2, cb, ALU.mult, rd, [tkey + "a"])
                tt(eng, tb, x1, sbb, ALU.mult, rd, [tkey + "b"])
                tt(eng, d3[:, :, half:2 * half], ta, tb, ALU.add, [tkey + "a", tkey + "b"], wr)
                cp(copy_eng if copy_eng is not None else eng, d3[:, :, 2 * half:hd], s3[:, :, 2 * half:hd], rd, wr)

            out_ops = []

            for I in range(NSUP):
                mark(100 + I)
                CV.reset()
                gen_pool[:] = [0, 1, 2, 3, 4, 5, 6, 7]
                sc = CV.f32(SEQ)
                xt = [sc[:, 0:D], sc[:, D:2 * D]]
                mbw = CV.f32(SEQ // 2)
                mbw2 = CV.f32(SEQ // 2)
                htmps = [sc[:, 2 * D:3 * D], mbw[:, 0:D]]
                hbfs = [sc[:, 3 * D:3 * D + D // 2].bitcast(BF16), mbw[:, D:D + D // 2].bitcast(BF16)]
                mbs = [mbw.bitcast(BF16), mbw2.bitcast(BF16)]
                Oraw = CV.f32(8 * 130).rearrange("p (h d) -> p h d", d=130)
                qraw = [CV.f32(512), CV.f32(512)]
                qnrm = CV.f32(512)
                qbf = CV.bf16(D)
                tmpc = CV.f32(64)
                tmpd = CV.f32(64)
                ssq = CV.f32(8)
                kraws = [CV.f32(256) for _ in range(NSUB)]
                knrms = [CV.f32(256) for _ in range(NSUB)]
                kbfs = [CV.bf16(256) for _ in range(NSUB)]
                iraw = CV.f32(324 * NSUB).rearrange("p (u c) -> p u c", c=324)
                kins = [CV.f32(64) for _ in range(NSUB)]
                kjunks = [CV.f32(64) for _ in range(NSUB)]
                kibfs = [CV.bf16(64) for _ in range(NSUB)]
                qibfs = [CV.bf16(256) for _ in range(NSUB)]
                wq = CV.f32(4 * NSUB).rearrange("p (u c) -> p u c", c=4)
                tmpas = [CV.f32(64) for _ in range(NSUB)]
                tmpbs = [CV.f32(64) for _ in range(NSUB)]
                st4s = [CV.f32(32) for _ in range(NSUB)]
                st4 = st4s[0]
                qT = CV.bf16(NSUB * D).rearrange("p (u h t) -> p u h t", h=8, t=128)
                qidxT = CV.bf16(NSUB * 512).rearrange("p (u h t) -> p u h t", h=4, t=128)
                szb = CV.bf16(NSUB * D).rearrange("p (u c) -> p u c", c=D)
                relu_t = [CV.f32(512) for _ in range(2)]
                PTb = [CV.bf16(512) for _ in range(3)]
                ogb = CV.bf16(D)
                ogT = CV.bf16(D).rearrange("p (h t) -> p h t", t=128)
                bis = CV.f32(16 + 2 * NITER)
                rl = CV.f32(8)
                bmx = CV.f32(256)
                top8 = CV.f32(8)
                sqjunk = CV.bf16(128)

                wkv, wkvk = load_w("kv")
                widx, widxk = load_w("idx")

                def p12(u):
                    j = I * NSUB + u
                    us = str(u)
                    xb = xt[u]
                    xkey = "A:xt" + us
                    hk = "A:htmp" + us
                    sq = ssq[:, 4 * u:4 * u + 4]
                    s4 = st4s[u]
                    dma(SP, xb, x_d[128 * j:128 * (j + 1), :], [], [xkey])
                    act(htmps[u], xb, AF.Square, [xkey], [hk], accum=sq[:, 0:1])
                    yield
                    rstd_pool(sq[:, 2:3], sq[:, 1:2], sq[:, 0:1], 1, 1.0 / D, [hk], ["A:ssq2" + us], "A:ssq1" + us)
                    stt(htmps[u], xb, sq[:, 2:3], modbc[:, D:2 * D], ALU.mult, ALU.mult,
                        [xkey, "A:ssq2" + us, "modbc", hk], [hk])
                    tt(DVE, hbfs[u], htmps[u], modbc[:, 0:D], ALU.add, [hk, "modbc"], ["A:hbf" + us])
                    yield
                    b = gen_bank()
                    pv = banks[b][:].bitcast(BF16).rearrange("p (k t) -> p k t", t=128)
                    for k in range(8):
                        tr(pv[:, k, :], hbfs[u][:, 128 * k:128 * (k + 1)], ident_b[:], ["A:hbf" + us, "ident_b"], [bk(b)])
                    yield
                    cp(ACT, hT[:, :, 128 * u:128 * (u + 1)], pv[:, 0:8, :], [bk(b)], ["hT" + us])
                    yield
                    b = proj_tok(wkv, wkvk, u, 512)
                    yield
                    cp(ACT, Vaug[:, j, :, 0:128], banks[b][:, 256:512].rearrange("p (g d) -> p g d", d=128),
                       [bk(b)], ["V%d" % j])
                    kr = kraws[u]
                    kn = knrms[u]
                    cp(ACT, kr, banks[b][:, 0:256], [bk(b)], ["A:kraw" + us])
                    for g in range(2):
                        act(kn[:, 128 * g:128 * (g + 1)], kr[:, 128 * g:128 * (g + 1)], AF.Square,
                            ["A:kraw" + us], ["A:knrm" + us], accum=s4[:, g:g + 1])
                    yield
                    rstd_pool(s4[:, 4:6], s4[:, 2:4], s4[:, 0:2], 2, 1.0 / 128, ["A:knrm" + us], ["A:st4c" + us],
                              "A:st4b" + us)
                    yield
                    for g in range(2):
                        stt(kn[:, 128 * g:128 * (g + 1)], kr[:, 128 * g:128 * (g + 1)], s4[:, 4 + g:5 + g],
                            pk2[:, KG:KG + 128], ALU.mult, ALU.mult,
                            ["A:kraw" + us, "A:st4c" + us, "pk2", "A:knrm" + us], ["A:knrm" + us])
                    rope_apply(DVE, kbfs[u], kn, 2, 128, 16, cosA, sinA, j, tmpas[u], tmpbs[u],
                               ["A:knrm" + us, "racos", "rasin"], ["A:kbf" + us], "A:tk" + us)
                    yield
                    b2 = gen_bank()
                    pv = banks[b2][:].bitcast(BF16).rearrange("p (k t) -> p k t", t=128)
                    for g in range(2):
                        tr(pv[:, g, :], kbfs[u][:, 128 * g:128 * (g + 1)], ident_b[:], ["A:kbf" + us, "ident_b"], [bk(b2)])
                    yield
                    cp(ACT, KT[:, :, 128 * j:128 * (j + 1)], pv[:, 0:2, :], [bk(b2)], ["K%d" % j])
                    yield
                    b = proj_tok(widx, widxk, u, 324)
                    ik = "A:iraw" + us
                    yield
                    cp(ACT, iraw[:, u, :], banks[b][:, 0:324], [bk(b)], [ik])
                    kx = iraw[:, u, 256:320]
                    kin = kins[u]
                    kjunk = kjunks[u]
                    tsc(DVE, kjunk, kx, 1.0, 0.0, ALU.mult, ALU.add, [ik], ["A:kjunk" + us], accum=s4[:, 6:7])
                    tsc(DVE, s4[:, 7:8], s4[:, 6:7], 1.0 / 64, None, ALU.mult, None, ["A:kjunk" + us], ["A:bn2" + us])
                    tsc(DVE, kin, kx, s4[:, 7:8], None, ALU.subtract, None, [ik, "A:bn2" + us], ["A:kin" + us])
                    yield
                    act(kjunk, kin, AF.Square, ["A:kin" + us, "A:kjunk" + us], ["A:kjunk2" + us], accum=s4[:, 8:9])
                    rstd_pool(s4[:, 10:11], s4[:, 9:10], s4[:, 8:9], 1, 1.0 / 64, ["A:kjunk2" + us], ["A:bn4" + us],
                              "A:bn3" + us)
                    yield
                    tsc(DVE, kin, kin, s4[:, 10:11], None, ALU.mult, None, ["A:kin" + us, "A:bn4" + us], ["A:kin" + us])
                    tt(DVE, kin, kin, pk2[:, IG:IG + 64], ALU.mult, ["A:kin" + us, "pk2"], ["A:kin" + us])
                    tt(DVE, kin, kin, pk2[:, IB:IB + 64], ALU.add, ["A:kin" + us, "pk2"], ["A:kin" + us])
                    rope_apply(DVE, kibfs[u], kin, 1, 64, 8, cosI, sinI, j, tmpas[u], tmpbs[u],
                               ["A:kin" + us, "ricos", "risin"], ["A:kibf" + us], "A:tk" + us)
                    yield
                    b2 = gen_bank()
                    pv = banks[b2][:].bitcast(BF16)
                    tr(pv[0:64, 0:128], kibfs[u], ident_b[:], ["A:kibf" + us, "ident_b"], [bk(b2)])
                    yield
                    cp(ACT, kidxT[:, 128 * j:128 * (j + 1)], pv[0:64, 0:128], [bk(b2)], ["KI%d" % j])
                    yield
                    rope_apply(DVE, qibfs[u], iraw[:, u, 0:256], 4, 64, 8, cosI, sinI, j, tmpas[u], tmpbs[u],
                               [ik, "ricos", "risin"], ["A:qibf" + us], "A:tk" + us)
                    yield
                    b3 = gen_bank()
                    pv = banks[b3][:].bitcast(BF16).rearrange("p (k t) -> p k t", t=128)
                    for hh in range(4):
                        tr(pv[0:64, hh, :], qibfs[u][:, 64 * hh:64 * (hh + 1)], ident_b[:], ["A:qibf" + us, "ident_b"],
                           [bk(b3)])
                    yield
                    cp(ACT, qidxT[0:64, u, :, :], pv[0:64, 0:4, :], [bk(b3)], ["A:qidxT%d" % u])
                    tsc(DVE, wq[:, u, :], iraw[:, u, 320:324], 0.0625, None, ALU.mult, None, [ik], ["A:wq%d" % u])
                    yield

                gens = [p12(u) for u in range(NSUB)]
                while gens:
                    for gq in list(gens):
                        try:
                            next(gq)
                        except StopIteration:
                            gens.remove(gq)
                HTK = ["hT%d" % uu for uu in range(NSUB)]
                dbg("hT", hT[:], HTK, BF16)
                mark(7)

                dbg("KT", KT[:, :, 0:TS], ["K%d" % jj for jj in range(NSUB)], BF16)
                dbg("V", Vaug[:, 0:NSUB, :, :], ["V%d" % jj for jj in range(NSUB)] + ["Vones"], BF16)
                dbg("kidxT", kidxT[:, 0:TS], ["KI%d" % jj for jj in range(NSUB)], BF16)
                dbg("qidxT", qidxT[0:64], ["A:qidxT%d" % uu for uu in range(NSUB)], BF16)
                dbg("wq", wq, ["A:wq%d" % uu for uu in range(NSUB)])
                def p3a():
                    gen_pool[:] = [2, 3, 4, 5, 6, 7]
                    blocks = [(half, u) for half in range(2) for u in range(NSUB)]
                    wts = {}

                    def stage_a(half, u):
                        if half not in wts:
                            wts[half] = load_w("q%d" % half)
                        wt, wkey = wts[half]
                        j = I * NSUB + u
                        b = proj_tok(wt, wkey, u, 512)
                        qr = qraw[(half * NSUB + u) % 2]
                        qk = "A:qraw%d" % ((half * NSUB + u) % 2)
                        cp(ACT, qr, banks[b][:], [bk(b)], [qk])
                        for hh in range(4):
                            act(sqjunk, qr[:, 128 * hh:128 * (hh + 1)], AF.Square,
                                [qk], ["A:sqjunk"], accum=st4[:, 16 + hh:17 + hh])
                        rstd_pool(st4[:, 24:28], st4[:, 20:24], st4[:, 16:20], 4, 1.0 / 128, ["A:sqjunk"], ["A:st4qc"],
                                  "A:st4qb")
                        for hh in range(4):
                            tsc(POOL, qnrm[:, 128 * hh:128 * (hh + 1)], qr[:, 128 * hh:128 * (hh + 1)],
                                st4[:, 24 + hh:25 + hh], 1.0, ALU.mult, ALU.mult, [qk, "A:st4qc", "A:qnrm"], ["A:qnrm"])
                        q3 = qnrm.rearrange("p (h d) -> p h d", d=128)
                        tt(POOL, q3, q3, pk2[:, QG:QG + 128].unsqueeze(1).to_broadcast([128, 4, 128]), ALU.mult,
                           ["A:qnrm", "pk2"], ["A:qnrm"])
                        qb = "A:qbf%d_%d" % (u, half)
                        rope_apply(POOL, qbf[:, 512 * half:512 * (half + 1)], qnrm, 4, 128, 16, cosA, sinA, j,
                                   tmpc, tmpd, ["A:qnrm", "racos", "rasin"], [qb], "A:tp")

                    def stage_b(half, u):
                        qb = "A:qbf%d_%d" % (u, half)
                        b2 = gen_bank()
                        pv = banks[b2][:].bitcast(BF16).rearrange("p (k t) -> p k t", t=128)
                        for hh in range(4):
                            tr(pv[:, hh, :], qbf[:, 512 * half + 128 * hh:512 * half + 128 * (hh + 1)], ident_b[:],
                               [qb, "ident_b"], [bk(b2)])
                        cp(ACT, qT[:, u, 4 * half:4 * half + 4, :], pv[:, 0:4, :], [bk(b2)], ["A:qT%d_%d" % (u, half)])

                    for i, (half, u) in enumerate(blocks):
                        stage_a(half, u)
                        yield
                        if i > 0:
                            stage_b(*blocks[i - 1])
                            yield
                    stage_b(*blocks[-1])
                    yield
                    for half in range(2):
                        wt, wkey = load_w("z%d" % half)
                        for u in range(NSUB):
                            b = proj_tok(wt, wkey, u, 512)
                            act(szb[:, u, 512 * half:512 * (half + 1)], banks[b][:], AF.Silu, [bk(b)],
                                ["A:sz%d_%d" % (u, half)])


                lo, hi, rng, mid, cnt, sgn, thr, sA, tot = [bis[:, i:i + 1] for i in range(9)]
                steps = bis[:, 16:16 + NITER]
                pw = bis[:, 16 + NITER:16 + 2 * NITER]
                for k in range(NITER):
                    memset(POOL, pw[:, k:k + 1], float(2.0 ** -(k + 1)), ["A:pw"])

                def idx_part(u):
                    j = I * NSUB + u
                    nk = j + 1
                    L = 128 * nk
                    for g0 in range(0, nk, 4):
                        g1 = min(nk, g0 + 4)
                        wdt = 128 * (g1 - g0)
                        rdk = ["KI%d" % kb for kb in range(g0, g1)]
                        for hh in range(4):
                            b = gen_bank()
                            mm(banks[b][:, 0:wdt], qidxT[0:64, u, hh, :], kidxT[:, 128 * g0:128 * g1], True, True,
                               ["A:qidxT%d" % u] + rdk, [bk(b)])
                            rt = relu_t[hh % 2]
                            rk = "A:relu%d" % (hh % 2)
                            act(rt[:, 0:wdt], banks[b][:, 0:wdt], AF.Relu, [bk(b)], [rk])
                            dst = sc[:, 128 * g0:128 * g1]
                            if hh == 0:
                                tsc(DVE, dst, rt[:, 0:wdt], wq[:, u, 0:1], None, ALU.mult, None,
                                    [rk, "A:wq%d" % u], ["A:sc"])
                            else:
                                stt(dst, rt[:, 0:wdt], wq[:, u, hh:hh + 1], dst, ALU.mult, ALU.add,
                                    [rk, "A:wq%d" % u, "A:sc"], ["A:sc"])
                    if j >= 2:
                        bs_ = j // 2
                        if bs_ > 1:
                            src3 = sc[:, 0:256 * bs_].rearrange("p (b s) -> p b s", s=bs_)
                            S.add(DVE, lambda e, src3=src3: e.tensor_reduce(bmx, src3, mybir.AxisListType.X, ALU.max),
                                  rd=["A:sc"], wr=["A:bmx"])
                            S.add(DVE, lambda e: e.tensor_reduce(lo, bmx, mybir.AxisListType.X, ALU.min),
                                  rd=["A:bmx"], wr=["A:lo"])
                        else:
                            S.add(DVE, lambda e: e.tensor_reduce(lo, sc[:, 0:256], mybir.AxisListType.X, ALU.min),
                                  rd=["A:sc"], wr=["A:lo"])
                    dg = sc[:, 128 * j:128 * (j + 1)]
                    S.add(POOL, lambda e, dg=dg: e.affine_select(dg, dg, [[-1, 128]], ALU.is_ge, BIGNEG,
                                                                base=0, channel_multiplier=1),
                          rd=["A:sc", "A:lo"], wr=["A:sc"])
                    if j >= 2:
                        S.add(DVE, lambda e, L=L: e.max(out=top8, in_=sc[:, 0:L]), rd=["A:sc"], wr=["A:top8"])
                        S.add(DVE, lambda e: e.tensor_reduce(hi, top8, mybir.AxisListType.X, ALU.min),
                              rd=["A:top8"], wr=["A:hi"])
                    if j >= 2:
                        tsc(DVE, lo, lo, 1.0e-6, None, ALU.subtract, None, ["A:lo"], ["A:lo"])
                        tt(DVE, rng, hi, lo, ALU.subtract, ["A:hi", "A:lo"], ["A:rng"])
                        tsc(DVE, steps, pw, rng, None, ALU.mult, None, ["A:pw", "A:rng"], ["A:steps"])
                        tt(DVE, mid, lo, steps[:, 0:1], ALU.add, ["A:lo", "A:steps"], ["A:mid"])
                    else:
                        memset(DVE, thr, -1.0e29, ["A:thr"])

                def bisect_gen(u, dve_frac):
                    j = I * NSUB + u
                    nk = j + 1
                    L = 128 * nk
                    if j < 2:
                        return
                    junk = mbs[j % 2]
                    kD = "A:mbD%d" % (j % 2)
                    kA = "A:mbA%d" % (j % 2)
                    nkD = nk if (nk < 6 or dve_frac >= 1.0) else max(1, int(round(dve_frac * nk)))
                    La = 128 * nkD
                    nA = L - La
                    for k in range(NITER):
                        tsc(DVE, junk[:, 0:La], sc[:, 0:La], mid, 0.0, ALU.is_gt, ALU.add,
                            ["A:sc", "A:mid"], [kD, "A:cnt"], accum=cnt)
                        if nA > 0:
                            act(junk[:, La:L], sc[:, La:L], AF.Sign, ["A:sc", "A:mid"], [kA, "A:sA"],
                                bias=mid, scale=-1.0, accum=sA)
                            stt(tot, sA, -0.5, cnt, ALU.mult, ALU.add, ["A:sA", "A:cnt"], ["A:tot"])
                            tsc(DVE, sgn, tot, float(TOPK) - 0.5 * nA, -0.5, ALU.is_ge, ALU.add, ["A:tot"], ["A:sgn"])
                        else:
                            tsc(DVE, sgn, cnt, float(TOPK), -0.5, ALU.is_ge, ALU.add, ["A:cnt"], ["A:sgn"])
                        if k < NITER - 1:
                            stt(mid, sgn, steps[:, k:k + 1], mid, ALU.mult, ALU.add,
                                ["A:sgn", "A:steps", "A:mid"], ["A:mid"])
                        else:
                            tsc(DVE, sgn, sgn, 0.5, None, ALU.subtract, None, ["A:sgn"], ["A:sgn"])
                            stt(thr, sgn, steps[:, k:k + 1], mid, ALU.mult, ALU.add,
                                ["A:sgn", "A:steps", "A:mid"], ["A:thr"])
                        yield

                def do_mask(u):
                    j = I * NSUB + u
                    L = 128 * (j + 1)
                    tsc(DVE, mbs[j % 2][:, 0:L], sc[:, 0:L], thr, NEG, ALU.is_le, ALU.mult, ["A:sc", "A:thr"],
                        ["A:mb%d" % (j % 2), "A:mbD%d" % (j % 2), "A:mbA%d" % (j % 2)])
                    if j == (debug.get("j", 1) if isinstance(debug, dict) else 1):
                        dbg("sc", sc[:, 0:L], ["A:sc"])
                        dbg("bis", bis, ["A:thr"])
                        dbg("mb", mbs[j % 2][:, 0:L], ["A:mb%d" % (j % 2)], BF16)

                def att_gen(u):
                    j = I * NSUB + u
                    nk = j + 1
                    mbj = mbs[j % 2]
                    mkey = "A:mb%d" % (j % 2)
                    scale = 128.0 ** -0.5
                    unitsl = [(g, kb) for g in range(2) for kb in range(nk)]

                    def s_stage(i):
                        g, kb = unitsl[i]
                        sb_ = 2 + (i % 2)
                        mm(banks[sb_][:], KT[:, g, 128 * kb:128 * (kb + 1)],
                           qT[:, u, 4 * g:4 * g + 4, :].rearrange("p h t -> p (h t)"), True, False,
                           ["K%d" % kb, "A:qT%d_%d" % (u, g)], [bk(sb_)])
                        mm(banks[sb_][:], mbj[:, 128 * kb:128 * (kb + 1)], I4_b[:], False, True,
                           [mkey, "A:mbD%d" % (j % 2), "A:mbA%d" % (j % 2), "I4_b"], [bk(sb_)])

                    s_stage(0)
                    for i, (g, kb) in enumerate(unitsl):
                        if i + 1 < len(unitsl):
                            s_stage(i + 1)
                        sb_ = 2 + (i % 2)
                        pt = PTb[i % 3]
                        pk = "A:PT%d" % (i % 3)
                        act(pt, banks[sb_][:], AF.Exp, [bk(sb_)], [pk], scale=scale)
                        for hh in range(4):
                            ob = 4 + hh
                            mm(banks[ob][:, 0:130], pt[:, 128 * hh:128 * (hh + 1)], Vaug[:, kb, g, :],
                               kb == 0, kb == nk - 1, [pk, "V%d" % kb, "Vones"], [bk(ob)])
                        if kb == nk - 1:
                            for hh in range(4):
                                h = 4 * g + hh
                                cp(ACT, Oraw[:, h, :], banks[4 + hh][:, 0:130], [bk(4 + hh)], ["A:Oraw%d" % h])
                        yield

                def att_fin(u):
                    for h in range(8):
                        recip(rl[:, h:h + 1], Oraw[:, h, 128:129], ["A:Oraw%d" % h], ["A:rl%d" % h])
                        stt(ogb[:, 128 * h:128 * (h + 1)], Oraw[:, h, 0:128], rl[:, h:h + 1],
                            szb[:, u, 128 * h:128 * (h + 1)], ALU.mult, ALU.mult,
                            ["A:Oraw%d" % h, "A:rl%d" % h, "A:sz%d_%d" % (u, h // 4)], ["A:ogb"])
                    b = gen_bank()
                    pv = banks[b][:].bitcast(BF16).rearrange("p (k t) -> p k t", t=128)
                    for h in range(8):
                        tr(pv[:, h, :], ogb[:, 128 * h:128 * (h + 1)], ident_b[:], ["A:ogb", "ident_b"], [bk(b)])
                    cp(ACT, ogT, pv[:, 0:8, :], [bk(b)], ["A:ogT"])
                    for c in range(2):
                        wt, wkey = wba[c]
                        b = gen_bank()
                        for h in range(8):
                            mm(banks[b][:], ogT[:, h, :], wt[:, h, :], h == 0, h == 7, ["A:ogT", wkey], [bk(b)])
                        cp(ACT, yatt[:, u, 512 * c:512 * (c + 1)], banks[b][:], [bk(b)], ["yatt"])

                gen_pool[:] = [0, 1]
                idx_part(0)
                ga = bisect_gen(0, BIS_DVE_FRAC)
                gb = p3a()
                alive = [ga, gb]
                reps = [2, 1]
                while any(x is not None for x in alive):
                    for ii in range(2):
                        for _ in range(reps[ii]):
                            if alive[ii] is None:
                                break
                            try:
                                next(alive[ii])
                            except StopIteration:
                                alive[ii] = None
                do_mask(0)
                dbg("qT", qT, ["A:qT%d_%d" % (uu, hf) for uu in range(NSUB) for hf in range(2)], BF16)
                dbg("szb", szb, ["A:sz%d_%d" % (uu, hf) for uu in range(NSUB) for hf in range(2)], BF16)
                mark(8)
                wba = [load_w("ba0"), load_w("ba1")]
                gen_pool[:] = [0, 1]
                for u in range(NSUB):
                    j = I * NSUB + u
                    mark(9)
                    if u + 1 < NSUB:
                        idx_part(u + 1)
                        ga = bisect_gen(u + 1, BIS1_DVE_FRAC)
                        gb = att_gen(u)
                        alive = [ga, gb]
                        reps = [1, max(1, int(round(2.0 * (j + 1) / NITER)))]
                        while any(x is not None for x in alive):
                            for ii in range(2):
                                for _ in range(reps[ii]):
                                    if alive[ii] is None:
                                        break
                                    try:
                                        next(alive[ii])
                                    except StopIteration:
                                        alive[ii] = None
                        do_mask(u + 1)
                    else:
                        for _ in att_gen(u):
                            pass
                    att_fin(u)

                dbg("yatt", yatt[:], ["yatt"])
                mark(10)
                CV.reset()
                gen_pool[:] = [0, 1]
                dtt = CV.f32(NSUB * 32).rearrange("p (u h) -> p u h", h=32)
                adt = CV.f32(NSUB * 32).rearrange("p (u h) -> p u h", h=32)
                eat = CV.f32(NSUB * 32).rearrange("p (u h) -> p u h", h=32)
                cdb = CV.f32(NSUB * 32).rearrange("p (u h) -> p u h", h=32)
                sp1 = CV.f32(32)
                sp2 = CV.f32(32)
                sp3 = CV.f32(32)
                uraw = [CV.f32(TS + 3) for _ in range(2)]
                cacc = [CV.f32(6 * TS).rearrange("p (c t) -> p c t", t=TS) for _ in range(2)]
                BTb = [CV.bf16(TS) for _ in range(2)]
                CTb = [CV.bf16(TS) for _ in range(2)]
                szs = [CV.f32(NSUB * 512).rearrange("p (u c) -> p u c", c=512) for _ in range(2)]
                Rm = [CV.f32(1024).rearrange("p (h l) -> p h l", l=128) for _ in range(NSUB)]
                decT = [CV.f32(1024).rearrange("p (h l) -> p h l", l=128) for _ in range(NSUB)]
                CBm = [CV.f32(128) for _ in range(NSUB)]
                MTb = [CV.bf16(1024).rearrange("p (h l) -> p h l", l=128) for _ in range(NSUB)]
                xstok = [CV.f32(512) for _ in range(NSUB)]
                xcb = [CV.bf16(512) for _ in range(NSUB)]
                xcdb = [CV.bf16(512) for _ in range(NSUB)]
                dtds = [CV.f32(8) for _ in range(NSUB)]
                Btok = [CV.bf16(128) for _ in range(NSUB)]
                y1 = CV.f32(512)
                y2 = CV.f32(512)
                ssn = CV.f32(4)
                ynb = CV.bf16(512)
                stmp = CV.f32(512)

                wt, wkey = dtw, "dtw"
                for u in range(NSUB):
                    b = proj_tok(wt, wkey, u, 32)
                    tt(DVE, sp1, banks[b][:, 0:32], pk2[:, DTB:DTB + 32], ALU.add, [bk(b), "pk2"], ["A:sp1"])
                    act(sp2, sp1, AF.Abs, ["A:sp1"], ["A:sp2"])
                    act(sp2, sp2, AF.Exp, ["A:sp2"], ["A:sp2"], scale=-1.0)
                    act(sp3, sp2, AF.Ln, ["A:sp2"], ["A:sp3"], bias=1.0)
                    stt(dtt[:, u, :], sp1, 0.0, sp3, ALU.max, ALU.add, ["A:sp1", "A:sp3"], ["A:dtt%d" % u])
                    tt(DVE, adt[:, u, :], dtt[:, u, :], abc[:], ALU.mult, ["A:dtt%d" % u, "abc"], ["A:adt%d" % u])
                    b = gen_bank()
                    mm(banks[b][:, 0:32], U_f[:], adt[:, u, :], True, True, ["U_f", "A:adt%d" % u], [bk(b)])
                    mm(banks[b][:, 32:64], ones_f[:], adt[:, u, :], True, True, ["ones_f", "A:adt%d" % u], [bk(b)])
                    act(eat[:, u, :], banks[b][:, 0:32], AF.Exp, [bk(b)], ["A:eat%d" % u])
                    act(cdb[:, u, :], banks[b][:, 32:64], AF.Exp, [bk(b)], ["A:cdb%d" % u])

                def Gstage(g):
                    st_ = g % 2
                    ss_ = str(st_)
                    wx, wxk = load_w("xs%d" % g)
                    wbc_, wbck = load_w("bc%d" % g)
                    srcs = [(wx, wxk, 128 * c) for c in range(4)] + [(wbc_, wbck, 0), (wbc_, wbck, 128)]
                    chidx = [4 * g + c for c in range(4)] + [16 + g, 20 + g]
                    for ci6, (wt_, wk_, coff) in enumerate(srcs):
                        b = gen_bank()
                        for k in range(8):
                            mm(banks[b][:, 0:TS], wt_[:, k, coff:coff + 128], hT[:, k, :], k == 0, k == 7,
                               [wk_] + HTK, [bk(b)])
                        ur = uraw[ci6 % 2]
                        ck = "A:uraw%d" % (ci6 % 2)
                        cc = chidx[ci6]
                        cp(POOL, ur[:, 0:3], ctail[:, cc, :], ["ctail%d" % cc], [ck])
                        cp(ACT, ur[:, 3:3 + TS], banks[b][:, 0:TS], [bk(b)], [ck])
                        cp(POOL, ctail[:, cc, :], ur[:, TS:TS + 3], [ck], ["ctail%d" % cc])
                        ca = cacc[st_][:, ci6, :]
                        ak = "A:cacc%d_%s" % (ci6, ss_)
                        act(ca, ur[:, 3:3 + TS], AF.Identity, [ck, "convp"], [ak],
                            bias=convp[:, cc, 4:5], scale=convp[:, cc, 3:4])
                        yield
                        for kk in range(3):
                            stt(ca, ur[:, kk:kk + TS], convp[:, cc, kk:kk + 1], ca,
                                ALU.mult, ALU.add, [ck, "convp", ak], [ak])
                        yield
                    for ci6 in range(6):
                        ca = cacc[st_][:, ci6, :]
                        ak = "A:cacc%d_%s" % (ci6, ss_)
                        if ci6 < 4:
                            act(ca, ca, AF.Silu, [ak], [ak])
                        elif ci6 == 4:
                            act(BTb[st_], ca, AF.Silu, [ak], ["A:BTb" + ss_])
                        else:
                            act(CTb[st_], ca, AF.Silu, [ak], ["A:CTb" + ss_])
                        if ci6 == 2:
                            yield
                    yield
                    wz, wzk = load_w("zs%d" % g)
                    for u in range(NSUB):
                        b = proj_tok(wz, wzk, u, 512)
                        act(szs[st_][:, u, :], banks[b][:], AF.Silu, [bk(b)], ["A:szs%d_%s" % (u, ss_)])
                        yield

                def units(g):
                    st_ = g % 2
                    ss_ = str(st_)
                    xsT = cacc[st_]
                    hs = slice(8 * g, 8 * g + 8)

                    def front(u):
                        tsl = slice(128 * u, 128 * (u + 1))
                        us = str(u)
                        tt(POOL, Rm[u], U_f[:].unsqueeze(1).to_broadcast([128, 8, 128]),
                           adt[:, u, hs].unsqueeze(2).to_broadcast([128, 8, 128]), ALU.mult,
                           ["U_f", "A:adt%d" % u], ["A:Rm" + us])
                        yield
                        for hf in range(2):
                            mm(banks[2 + hf][:], G_f[:], Rm[u][:, 4 * hf:4 * hf + 4, :].rearrange("p h l -> p (h l)"),
                               True, True, ["G_f", "A:Rm" + us], [bk(2 + hf)])
                            act(decT[u][:, 4 * hf:4 * hf + 4, :].rearrange("p h l -> p (h l)"), banks[2 + hf][:], AF.Exp,
                                [bk(2 + hf)], ["A:decT%d_%s" % (hf, us)])
                        yield
                        mm(banks[7][:, 0:128], BTb[st_][:, tsl], CTb[st_][:, tsl], True, True,
                           ["A:BTb" + ss_, "A:CTb" + ss_], [bk(7)])
                        tt(DVE, CBm[u], banks[7][:, 0:128], U_f[:], ALU.mult, [bk(7), "U_f"], ["A:CBm" + us])
                        tt(DVE, MTb[u], decT[u], CBm[u].unsqueeze(1).to_broadcast([128, 8, 128]), ALU.mult,
                           ["A:decT0_" + us, "A:decT1_" + us, "A:CBm" + us], ["A:MTb" + us])
                        yield
                        b = gen_bank()
                        for c in range(4):
                            tr(banks[b][:, 128 * c:128 * (c + 1)], xsT[:, c, tsl], ident_f[:],
                               ["A:cacc%d_%s" % (c, ss_), "ident_f"], [bk(b)])
                        yield
                        cp(ACT, xstok[u], banks[b][:], [bk(b)], ["A:xstok" + us])
                        x3 = xstok[u].rearrange("p (h d) -> p h d", d=64)
                        tt(POOL, xcb[u].rearrange("p (h d) -> p h d", d=64), x3,
                           dtt[:, u, hs].unsqueeze(2).to_broadcast([128, 8, 64]), ALU.mult,
                           ["A:xstok" + us, "A:dtt%d" % u], ["A:xcb" + us])
                        tt(DVE, dtds[u], decT[u][:, :, 127], dtt[:, u, hs], ALU.mult,
                           ["A:decT0_" + us, "A:decT1_" + us, "A:dtt%d" % u], ["A:dtds" + us])
                        tt(POOL, xcdb[u].rearrange("p (h d) -> p h d", d=64), x3,
                           dtds[u].unsqueeze(2).to_broadcast([128, 8, 64]), ALU.mult,
                           ["A:xstok" + us, "A:dtds" + us], ["A:xcdb" + us])
                        yield
                        b = gen_bank()
                        pvb = banks[b][:].bitcast(BF16)
                        tr(pvb[:, 0:128], BTb[st_][:, tsl], ident_b[:], ["A:BTb" + ss_, "ident_b"], [bk(b)])
                        yield
                        cp(ACT, Btok[u], pvb[:, 0:128], [bk(b)], ["A:Btok" + us])
                        yield

                    def back(u):
                        tsl = slice(128 * u, 128 * (u + 1))
                        us = str(u)
                        x3 = xstok[u].rearrange("p (h d) -> p h d", d=64)
                        bY, bO, bS = [(4, 5, 6), (2, 3, 7)][u % 2]
                        for hh in range(8):
                            mm(banks[bY][:, 64 * hh:64 * (hh + 1)], MTb[u][:, hh, :], xcb[u][:, 64 * hh:64 * (hh + 1)],
                               True, True, ["A:MTb" + us, "A:xcb" + us], [bk(bY)])
                        mm(banks[bO][:], CTb[st_][:, tsl], Stb[:, g, :], True, True, ["A:CTb" + ss_, "Stb%d" % g], [bk(bO)])
                        mm(banks[bS][:], Btok[u], xcdb[u], True, True, ["A:Btok" + us, "A:xcdb" + us], [bk(bS)])
                        tt(POOL, y2.rearrange("p (h d) -> p h d", d=64), x3,
                           pk2[:, DSK + 8 * g:DSK + 8 * g + 8].unsqueeze(2).to_broadcast([128, 8, 64]), ALU.mult,
                           ["A:xstok" + us, "pk2"], ["A:y2"])
                        tt(DVE, y1.rearrange("p (h d) -> p h d", d=64), banks[bO][:].rearrange("p (h d) -> p h d", d=64),
                           eat[:, u, hs].unsqueeze(2).to_broadcast([128, 8, 64]), ALU.mult,
                           [bk(bO), "A:eat%d" % u], ["A:y1"])
                        tt(DVE, y1, y1, banks[bY][:], ALU.add, ["A:y1", bk(bY)], ["A:y1"])
                        tt(DVE, y1, y1, y2, ALU.add, ["A:y1", "A:y2"], ["A:y1"])
                        tt(DVE, y1, y1, szs[st_][:, u, :], ALU.mult, ["A:y1", "A:szs%d_%s" % (u, ss_)], ["A:y1"])
                        yield
                        act(y2, y1, AF.Square, ["A:y1", "A:y2"], ["A:y2"], accum=ssn[:, 0:1])
                        rstd_pool(ssn[:, 2:3], ssn[:, 1:2], ssn[:, 0:1], 1, 1.0 / 512, ["A:y2"], ["A:ssn2"], "A:ssn1")
                        yield
                        tsc(DVE, ynb, y1, ssn[:, 2:3], None, ALU.mult, None, ["A:y1", "A:ssn2"], ["A:ynb"])
                        tt(DVE, stmp.rearrange("p (h d) -> p h d", d=64), St[:, g, :].rearrange("p (h d) -> p h d", d=64),
                           cdb[:, u, hs].unsqueeze(2).to_broadcast([128, 8, 64]), ALU.mult,
                           ["St%d" % g, "A:cdb%d" % u], ["A:stmp"])
                        tt(DVE, St[:, g, :], stmp, banks[bS][:], ALU.add, ["A:stmp", bk(bS)], ["St%d" % g])
                        cp(ACT, Stb[:, g, :], St[:, g, :], ["St%d" % g], ["Stb%d" % g])
                        b = gen_bank()
                        pv = banks[b][:].bitcast(BF16).rearrange("p (k t) -> p k t", t=128)
                        for c in range(4):
                            tr(pv[:, c, :], ynb[:, 128 * c:128 * (c + 1)], ident_b[:], ["A:ynb", "ident_b"], [bk(b)])
                        yield
                        cp(ACT, ynT[:, 4 * g:4 * g + 4, tsl], pv[:, 0:4, :], [bk(b)], ["ynT"])
                        yield

                    for u in range(NSUB):
                        yield from front(u)
                    for u in range(NSUB):
                        yield from back(u)

                def interleave(a, b):
                    gens = [x for x in (a, b) if x is not None]
                    while gens:
                        for x in list(gens):
                            try:
                                next(x)
                            except StopIteration:
                                gens.remove(x)

                GTOP = ARENA_F - 2 * NSUB * D
                assert CV.off <= GTOP, CV.off
                sga = arena[:, GTOP:GTOP + NSUB * D].rearrange("p (u c) -> p u c", c=D)
                sgs = arena[:, GTOP + NSUB * D:GTOP + 2 * NSUB * D].rearrange("p (u c) -> p u c", c=D)

                def gates_gen():
                    gak = ["G:sga%d_%d" % (u, c) for c in range(2) for u in range(NSUB)]
                    gsk = ["G:sgs%d_%d" % (u, c) for c in range(2) for u in range(NSUB)]
                    for c in range(2):
                        wt, wkey = load_w("ga%d" % c)
                        for u in range(NSUB):
                            b = proj_tok(wt, wkey, u, 512)
                            cp(ACT, sga[:, u, 512 * c:512 * (c + 1)], banks[b][:], [bk(b)],
                               ["G:sga%d_%d" % (u, c), "A:gguard"])
                            yield
                    for c in range(2):
                        wt, wkey = load_w("gs%d" % c)
                        for u in range(NSUB):
                            b = proj_tok(wt, wkey, u, 512)
                            cp(ACT, sgs[:, u, 512 * c:512 * (c + 1)], banks[b][:], [bk(b)],
                               ["G:sgs%d_%d" % (u, c), "A:gguard"])
                            yield
                    sga_all = arena[:, GTOP:GTOP + NSUB * D]
                    sgs_all = arena[:, GTOP + NSUB * D:GTOP + 2 * NSUB * D]
                    act(sga_all, sga_all, AF.Sigmoid, gak, gak)
                    act(sgs_all, sgs_all, AF.Sigmoid, gsk, gsk)
                    yield
                    tt(DVE, sga_all, sga_all, yatt[:].rearrange("p u c -> p (u c)"), ALU.mult, gak + ["yatt"], gak)
                    yield

                interleave(Gstage(0), None)
                for g in range(4):
                    interleave(units(g), Gstage(g + 1) if g < 3 else gates_gen())

                dbg("ynT", ynT[:], ["ynT"], BF16)
                dbg("St", St[:], ["St%d" % gg for gg in range(4)])
                mark(11)
                CV.reset()
                gen_pool[:] = [0, 1, 2, 3, 4, 5, 6, 7]
                m2s = [CV.f32(512) for _ in range(2)]
                mbf = CV.bf16(NSUB * D).rearrange("p (u c) -> p u c", c=D)
                mT = CV.bf16(NSUB * D).rearrange("p (u k t) -> p u k t", k=8, t=128)
                xr = [CV.f32(D) for _ in range(2)]
                ot = [CV.f32(D) for _ in range(2)]
                assert NSUB <= 2 and CV.off <= GTOP
                for u in range(NSUB):
                    j = I * NSUB + u
                    dma(ACT, xr[u % 2], x_d[128 * j:128 * (j + 1), :], [], ["A:xr%d" % (u % 2)])

                mi = 0
                for c in range(2):
                    w0, w0k = load_w("bs0%d" % c)
                    w1, w1k = load_w("bs1%d" % c)
                    for u in range(NSUB):
                        m2 = m2s[mi % 2]
                        m2k = "A:m2_%d" % (mi % 2)
                        mi += 1
                        b2 = gen_bank()
                        for kc in range(16):
                            wt_ = w0 if kc < 8 else w1
                            wk_ = w0k if kc < 8 else w1k
                            mm(banks[b2][:], ynT[:, kc, 128 * u:128 * (u + 1)], wt_[:, kc % 8, :], kc == 0, kc == 15,
                               ["ynT", wk_], [bk(b2)])
                        tt(DVE, m2, sgs[:, u, 512 * c:512 * (c + 1)], banks[b2][:], ALU.mult,
                           ["G:sgs%d_%d" % (u, c), bk(b2)], [m2k])
                        tt(DVE, mbf[:, u, 512 * c:512 * (c + 1)], m2, sga[:, u, 512 * c:512 * (c + 1)], ALU.add,
                           [m2k, "G:sga%d_%d" % (u, c)], ["A:mbf%d" % u])
                for u in range(NSUB):
                    b = gen_bank()
                    pv = banks[b][:].bitcast(BF16).rearrange("p (k t) -> p k t", t=128)
                    for k in range(8):
                        tr(pv[:, k, :], mbf[:, u, 128 * k:128 * (k + 1)], ident_b[:], ["A:mbf%d" % u, "ident_b"], [bk(b)])
                    cp(ACT, mT[:, u, :, :], pv[:, 0:8, :], [bk(b)], ["A:mT%d" % u])
                wo = [load_w("wo0"), load_w("wo1")]
                for u in range(NSUB):
                    j = I * NSUB + u
                    xk = "A:xr%d" % (u % 2)
                    okk = "A:ot%d" % (u % 2)
                    for c in range(2):
                        wt, wkey = wo[c]
                        b = gen_bank()
                        for k in range(8):
                            mm(banks[b][:], mT[:, u, k, :], wt[:, k, :], k == 0, k == 7, ["A:mT%d" % u, wkey], [bk(b)])
                        tt(DVE, ot[u % 2][:, 512 * c:512 * (c + 1)], banks[b][:], modbc[:, 2 * D + 512 * c:2 * D + 512 * (c + 1)],
                           ALU.mult, [bk(b), "modbc"], [okk])
                    tt(DVE, ot[u % 2], ot[u % 2], xr[u % 2], ALU.add, [okk, xk], [okk])
                    out_ops.append(dma(ACT, out_d[128 * j:128 * (j + 1), :], ot[u % 2], [okk], ["out%d" % j]))


        except _Stop:
            pass

        S.add(SP, None, rd=["out%d" % j for j in range(NT)] + [k for k in S.reg if k.startswith("dbg_")], wr=[])

        with nc.Block() as block:
            S.emit_all(nc, block, esem, dsem)
    return nc


def _prep_inputs(inputs):
    f = lambda a: np.ascontiguousarray(np.asarray(a), dtype=np.float32)
    x = f(inputs["x"])
    c = f(inputs["c"])
    pos = np.ascontiguousarray(np.asarray(inputs["positions"]), dtype=np.int32)
    ada_w = f(inputs["ada_w"])[0]
    pk1 = np.concatenate([f(inputs["ada_b"])[0], f(inputs["norm_g"])[0]])[None, :]
    pk2 = np.zeros((1, 512), np.float32)
    parts = [("q_norm_g", 0), ("k_norm_g", 128), ("idx_k_ln_g", 256), ("idx_k_ln_b", 320), ("dt_bias", 384),
             ("a_log", 416), ("d_skip", 448)]
    for name, off in parts:
        v = f(inputs[name])[0]
        pk2[0, off:off + v.shape[0]] = v
    conv_w = f(inputs["conv_w"])[0]
    conv_b = f(inputs["conv_b"])[0]
    convp = np.concatenate([conv_w.T, conv_b[:, None]], axis=1)
    convp = np.ascontiguousarray(convp.reshape(24, 128, 5).transpose(1, 0, 2))
    gs16 = np.ascontiguousarray(f(inputs["ssm_norm_g"])[0].reshape(16, 128).T)
    shared = {
        "ada_w": ada_w, "pk1": np.ascontiguousarray(pk1), "pk2": pk2, "convp": convp, "gs16": gs16,
        "w_in": f(inputs["w_in"])[0], "w_ba": f(inputs["w_branch_att"])[0],
        "w_bs": f(inputs["w_branch_ssm"])[0], "w_out": f(inputs["w_out"])[0],
    }
    maps = []
    for b in range(8):
        m = dict(shared)
        m["x"] = x[b]
        m["c2"] = np.ascontiguousarray(c[b].reshape(8, 128).T)
        m["pos2"] = np.ascontiguousarray(pos[b].reshape(NT, 128).T)
        maps.append(m)
    return maps


def kernel(**inputs):
    maps = _prep_inputs(inputs)
    nc = build_nc()
    res = run_bass_kernel_spmd(nc, maps, core_ids=list(range(8)))
    out = np.stack([np.asarray(r["out"], dtype=np.float32) for r in res.results], axis=0)
    return out
```

```python
import numpy as np
import ml_dtypes
import concourse.bass as bass
import concourse.mybir as mybir
from concourse.bass_utils import run_bass_kernel_spmd

F32 = mybir.dt.float32
BF16 = mybir.dt.bfloat16
I32 = mybir.dt.int32
AF = mybir.ActivationFunctionType
ALU = mybir.AluOpType

SEQ = 4096
D = 1024
NT = SEQ // 128
NSUB = 2
NSUP = NT // NSUB
TS = NSUB * 128
IN_DIM = 10084
NITER = 16
BIS1_DVE_FRAC = 0.6
BIS_DVE_FRAC = 0.5
TOPK = 256
NEG = -30000.0
BIGNEG = -1.0e30

PE, ACT, DVE, POOL, SP = 0, 1, 2, 3, 4
NDMA = 24


class Op:
    __slots__ = ("eng", "idx", "emit", "waits", "flag", "clk", "dma", "val")

    def __init__(self, eng, idx, emit):
        self.eng = eng
        self.idx = idx
        self.emit = emit
        self.waits = []
        self.flag = False
        self.clk = None
        self.dma = None
        self.val = 0


class Sched:
    def __init__(self):
        self.ops = [[] for _ in range(5)]
        self.clk = [[-1] * 5 for _ in range(5)]
        self.dclk = [dict() for _ in range(5)]
        self.reg = {}
        self.retired = []
        self.dma_rr = 0
        self.dma_last = [None] * NDMA
        self.dma_uses = [0] * NDMA

    def _need(self, op, p):
        e = op.eng
        clk = self.clk[e]
        if p.dma is None:
            if p.eng == PE and e == PE:
                return
            if clk[p.eng] >= p.idx:
                return
        else:
            k, u = p.dma
            if self.dclk[e].get(k, 0) >= u:
                return
            self.dclk[e][k] = u
        p.flag = True
        op.waits.append(p)
        pc = p.clk
        for i in range(5):
            if pc[i] > clk[i]:
                clk[i] = pc[i]

    def _state(self, key):
        st = self.reg.get(key)
        if st is None:
            if key.startswith("A:"):
                st = [None, {}, list(self.retired)]
            else:
                st = [None, {}, []]
            self.reg[key] = st
        return st

    def add(self, eng, emit, rd=(), wr=(), dma=False):
        op = Op(eng, len(self.ops[eng]), emit)
        deps = []
        for r in rd:
            st = self._state(r)
            if st[0] is not None:
                deps.append(st[0])
        for w in wr:
            st = self._state(w)
            if st[0] is not None:
                deps.append(st[0])
            deps.extend(st[1].values())
            deps.extend(st[2])
        for p in deps:
            self._need(op, p)
        if dma:
            k = self.dma_rr
            self.dma_rr = (k + 1) % NDMA
            prev = self.dma_last[k]
            if prev is not None:
                self._need(op, prev)
            self.dma_uses[k] += 1
            op.dma = (k, self.dma_uses[k])
            op.flag = True
            self.dma_last[k] = op
            op.clk = tuple(self.clk[eng])
        else:
            c = list(self.clk[eng])
            c[eng] = op.idx
            op.clk = tuple(c)
        for r in rd:
            st = self.reg[r]
            if dma:
                st[2].append(op)
            else:
                st[1][eng] = op
        for w in wr:
            st = self.reg[w]
            st[0] = op
            st[1] = {}
            st[2] = []
        self.ops[eng].append(op)
        return op

    def retire_arena(self):
        ops = list(self.retired)
        for key in [k for k in self.reg if k.startswith("A:")]:
            st = self.reg.pop(key)
            if st[0] is not None:
                ops.append(st[0])
            ops.extend(st[1].values())
            ops.extend(st[2])
        best = {}
        dm = []
        for p in ops:
            if p.dma is not None:
                dm.append(p)
            elif p.eng not in best or best[p.eng].idx < p.idx:
                best[p.eng] = p
        self.retired = list(best.values()) + dm[-64:]

    def emit_all(self, nc, block, esem, dsem):
        for e in range(5):
            cnt = 0
            for op in self.ops[e]:
                if op.flag and op.dma is None:
                    cnt += 1
                    op.val = cnt

        def run(e, h):
            for op in self.ops[e]:
                best = {}
                for p in op.waits:
                    if p.dma is None:
                        key = ("e", p.eng)
                        v = p.val
                    else:
                        key = ("d", p.dma[0])
                        v = 16 * p.dma[1]
                    if best.get(key, -1) < v:
                        best[key] = v
                for (kind, i), v in best.items():
                    h.wait_ge(esem[i] if kind == "e" else dsem[i], v)
                if op.emit is None:
                    continue
                ins = op.emit(h)
                if op.flag:
                    if op.dma is None:
                        ins.then_inc(esem[e], 1)
                    else:
                        ins.then_inc(dsem[op.dma[0]], 16)

        block.tensor(lambda h: run(PE, h))
        block.scalar(lambda h: run(ACT, h))
        block.vector(lambda h: run(DVE, h))
        block.gpsimd(lambda h: run(POOL, h))
        block.sync(lambda h: run(SP, h))


def _cw_consts():
    two_pi = 2.0 * np.pi
    c1 = np.float32(6.28125)
    r = two_pi - float(c1)
    c2 = np.float32(r)
    b = c2.view(np.uint32) & np.uint32(0xFFFFF000)
    c2 = np.array(b, dtype=np.uint32).view(np.float32)
    c3 = np.float32(r - float(c2))
    return float(c1), float(c2), float(c3)


def _chunks():
    ch = []
    def w_in(name, c0, w):
        ch.append((name, "w_in", 0, c0, w))
    w_in("kv", 1024, 512)
    w_in("idx", 2560, 324)
    w_in("q0", 0, 512)
    w_in("q1", 512, 512)
    w_in("z0", 1536, 512)
    w_in("z1", 2048, 512)
    w_in("dt", 8004, 32)
    for g in range(4):
        w_in("zs%d" % g, 2884 + 512 * g, 512)
        w_in("xs%d" % g, 4932 + 512 * g, 512)
    for g in range(4):
        ch.append(("bc%d" % g, "w_in", 0, [6980 + 128 * g, 7492 + 128 * g], 128))
    w_in("ga0", 8036, 512)
    w_in("ga1", 8548, 512)
    w_in("gs0", 9060, 512)
    w_in("gs1", 9572, 512)
    for c in range(2):
        ch.append(("ba%d" % c, "w_ba", 0, 512 * c, 512))
    for kh in range(2):
        for c in range(2):
            ch.append(("bs%d%d" % (kh, c), "w_bs", 1024 * kh, 512 * c, 512))
    for c in range(2):
        ch.append(("wo%d" % c, "w_out", 0, 512 * c, 512))
    return ch


CHUNKS = _chunks()
CHIDX = {c[0]: i for i, c in enumerate(CHUNKS)}
CHW = {c[0]: (c[4] * len(c[3]) if isinstance(c[3], list) else c[4]) for c in CHUNKS}


class _Stop(Exception):
    pass


def build_nc(debug=None, stop=None):
    nc = bass.Bass("TRN2", target_bir_lowering=False)
    S = Sched()

    def mark(n):
        if stop is not None and stop == n:
            raise _Stop()

    def din(name, shape, dt=F32):
        return nc.dram_tensor(name, shape, dt, kind="ExternalInput").ap()

    x_d = din("x", [SEQ, D])
    c_d = din("c2", [128, 8])
    pos_d = din("pos2", [128, NT], I32)
    adaw_d = din("ada_w", [D, 3 * D])
    pk1_d = din("pk1", [1, 4096])
    pk2_d = din("pk2", [1, 512])
    convp_d = din("convp", [128, 24, 5])
    gs16_d = din("gs16", [128, 16])
    wsrc = {
        "w_in": din("w_in", [D, IN_DIM]),
        "w_ba": din("w_ba", [D, D]),
        "w_bs": din("w_bs", [2 * D, D]),
        "w_out": din("w_out", [D, D]),
    }
    out_d = nc.dram_tensor("out", [SEQ, D], F32, kind="ExternalOutput").ap()
    wsc_d = nc.dram_tensor("wscratch", [len(CHUNKS), 128, 8 * 512], BF16, kind="Internal").ap()
    dbg_d = {}

    from contextlib import ExitStack
    es = ExitStack()
    with es:
        def sb(name, shape, dt=F32):
            return es.enter_context(nc.sbuf_tensor(name, shape, dt))

        def ps(name, shape, dt=F32):
            return es.enter_context(nc.psum_tensor(name, shape, dt))

        esem = [es.enter_context(nc.semaphore("es%d" % i)) for i in range(5)]
        dsem = [es.enter_context(nc.semaphore("ds%d" % i)) for i in range(NDMA)]

        ident_f = sb("ident_f", [128, 128])
        ident_b = sb("ident_b", [128, 128], BF16)
        U_f = sb("U_f", [128, 128])
        G_f = sb("G_f", [128, 128])
        ones_f = sb("ones_f", [128, 128])
        mhalf = sb("mhalf", [128, 8])
        I4_b = sb("I4_b", [128, 512], BF16)
        modbc = sb("modbc", [128, 3 * D])
        pk2 = sb("pk2s", [128, 512])
        abc = sb("abc", [128, 32])
        convp = sb("convps", [128, 24, 5])
        gs16 = sb("gs16s", [128, 16])
        cosA = sb("cosA", [128, NT, 16])
        sinA = sb("sinA", [128, NT, 16])
        cosI = sb("cosI", [128, NT, 8])
        sinI = sb("sinI", [128, NT, 8])
        hT = sb("hT", [128, 8, TS], BF16)
        KT = sb("KT", [128, 2, SEQ], BF16)
        Vaug = sb("Vaug", [128, NT, 2, 130], BF16)
        kidxT = sb("kidxT", [64, SEQ], BF16)
        St = sb("St", [128, 4, 512])
        Stb = sb("Stb", [128, 4, 512], BF16)
        ctail = sb("ctail", [128, 24, 3])
        NRING = 4
        ring = [sb("ring%d" % i, [128, 8, 512], BF16) for i in range(NRING)]
        yatt = sb("yatt", [128, NSUB, D])
        dtw = sb("dtw", [128, 8, 32], BF16)
        ynT = sb("ynT", [128, 16, TS], BF16)
        ARENA_F = 20192
        arena = sb("arena", [128, ARENA_F])

        banks = [ps("bank%d" % i, [128, 512]) for i in range(8)]

        class Carver:
            def __init__(self):
                self.off = 0

            def reset(self):
                self.off = 0
                S.retire_arena()

            def f32(self, n):
                a = arena[:, self.off:self.off + n]
                self.off += n
                assert self.off <= ARENA_F, self.off
                return a

            def bf16(self, n):
                w = (n + 1) // 2
                a = arena[:, self.off:self.off + w].bitcast(BF16)
                self.off += w
                assert self.off <= ARENA_F, self.off
                return a

        CV = Carver()

        gen_pool = [0, 1]
        gen_i = [0]

        def gen_bank():
            b = gen_pool[gen_i[0] % len(gen_pool)]
            gen_i[0] += 1
            return b

        def bk(i):
            return "bank%d" % i

        rr = [0]

        def dma(eng, out, in_, rd, wr):
            h = {SP: None}
            return S.add(eng, lambda e: e.dma_start(out=out, in_=in_), rd=rd, wr=wr, dma=True)

        def mm(out, lhsT, rhs, start, stop, rd, wr):
            return S.add(PE, lambda e: e.matmul(out, lhsT, rhs, start=start, stop=stop), rd=rd, wr=wr)

        def tr(out, in_, ident, rd, wr):
            return S.add(PE, lambda e: e.transpose(out, in_, ident), rd=rd, wr=wr)

        def act(out, in_, func, rd, wr, bias=None, scale=None, accum=None):
            kw = {}
            if bias is not None:
                kw["bias"] = bias
            if scale is not None:
                kw["scale"] = scale
            if accum is not None:
                kw["accum_out"] = accum
            return S.add(ACT, lambda e: e.activation(out, in_, func, **kw), rd=rd, wr=wr)

        def tsc(eng, out, in0, s1, s2, op0, op1, rd, wr, accum=None):
            kw = {}
            if accum is not None:
                kw["accum_out"] = accum
            if op1 is None:
                return S.add(eng, lambda e: e.tensor_scalar(out, in0, s1, None, op0, **kw), rd=rd, wr=wr)
            return S.add(eng, lambda e: e.tensor_scalar(out, in0, s1, s2, op0, op1, **kw), rd=rd, wr=wr)

        def stt(out, in0, sc, in1, op0, op1, rd, wr):
            return S.add(DVE, lambda e: e.scalar_tensor_tensor(out, in0, sc, in1, op0, op1), rd=rd, wr=wr)

        def tt(eng, out, in0, in1, op, rd, wr):
            return S.add(eng, lambda e: e.tensor_tensor(out, in0, in1, op), rd=rd, wr=wr)

        def cp(eng, out, in_, rd, wr, psum=False):
            if eng == DVE and psum:
                return S.add(DVE, lambda e: e.tensor_scalar(out, in_, 1.0, None, ALU.mult), rd=rd, wr=wr)
            if eng == ACT:
                return S.add(ACT, lambda e: e.copy(out, in_), rd=rd, wr=wr)
            return S.add(eng, lambda e: e.tensor_copy(out, in_), rd=rd, wr=wr)

        def memset(eng, ap, val, wr):
            return S.add(eng, lambda e: e.memset(ap, val), rd=(), wr=wr)

        def recip(out, in_, rd, wr):
            return S.add(DVE, lambda e: e.reciprocal(out, in_), rd=rd, wr=wr)

        def rstd_pool(out, tmp, in_, n, scale, rd, wr, tkey):
            tsc(POOL, tmp, in_, scale, 1e-6, ALU.mult, ALU.add, rd, [tkey])
            tt(POOL, out, tmp, mhalf[:, 0:n], ALU.pow, [tkey, "mhalf"], wr)

        def dbg(name, ap_sb, rd, dt=F32):
            if not debug or name in dbg_d:
                return
            shp = list(ap_sb.shape)
            t = nc.dram_tensor("dbg_" + name, shp, dt, kind="ExternalOutput").ap()
            dbg_d[name] = t
            dma(SP, t, ap_sb, rd=rd, wr=["dbg_" + name])

        try:
            memset(POOL, ones_f[:], 1.0, ["ones_f"])
            memset(POOL, mhalf[:], -0.5, ["mhalf"])
            S.add(POOL, lambda e: e.affine_select(ident_f[:], ones_f[:], [[-1, 128]], ALU.is_equal, 0.0,
                                                  base=0, channel_multiplier=1), rd=["ones_f"], wr=["ident_f"])
            S.add(POOL, lambda e: e.affine_select(U_f[:], ones_f[:], [[1, 128]], ALU.is_ge, 0.0,
                                                  base=0, channel_multiplier=-1), rd=["ones_f"], wr=["U_f"])
            S.add(POOL, lambda e: e.affine_select(G_f[:], ones_f[:], [[-1, 128]], ALU.is_gt, 0.0,
                                                  base=0, channel_multiplier=1), rd=["ones_f"], wr=["G_f"])
            cp(POOL, ident_b[:], ident_f[:], ["ident_f"], ["ident_b"])
            for i in range(4):
                cp(POOL, I4_b[:, 128 * i:128 * (i + 1)], ident_f[:], ["ident_f"], ["I4_b"])
            memset(POOL, St[:], 0.0, ["St%d" % g for g in range(4)])
            memset(POOL, Stb[:], 0.0, ["Stb%d" % g for g in range(4)])
            memset(POOL, ctail[:], 0.0, ["ctail%d" % c for c in range(24)])
            memset(POOL, Vaug[:, :, :, 128:130], 1.0, ["Vones"])

            mark(1)
            CV.off = 0
            wbf_base = CV.off
            pk1 = CV.f32(4096)
            c2s = CV.f32(8)
            scs = CV.f32(8)
            scb = CV.f32(8 * 128).rearrange("p (k m) -> p k m", m=128)
            posi = CV.f32(NT).bitcast(I32)
            posf = CV.f32(NT)
            inv = CV.f32(16)
            ang = CV.f32(NT * 16)
            nn_ = CV.f32(NT * 16)
            rr_ = CV.f32(NT * 16)
            aa_ = CV.f32(NT * 16)
            adaw = [CV.f32(8 * 512).rearrange("p (k c) -> p k c", c=512) for _ in range(2)]
            wst = adaw
            wbf = [arena[:, wbf_base + 2048 * i:wbf_base + 2048 * (i + 1)].bitcast(BF16).rearrange("p (k c) -> p k c", c=512)
                   for i in range(2)]
            PK1 = ["A:wbf0", "A:wbf1"]

            dma(SP, pk1, pk1_d.partition_broadcast(128), [], PK1)
            dma(SP, pk2[:], pk2_d.partition_broadcast(128), [], ["pk2"])
            dma(SP, c2s, c_d, [], ["A:c2s"])
            dma(SP, posi, pos_d, [], ["A:posi"])
            dma(SP, convp[:], convp_d, [], ["convp"])
            dma(SP, gs16[:], gs16_d, [], ["gs16"])
            QG, KG, IG, IB, DTB, ALOG, DSK = 0, 128, 256, 320, 384, 416, 448

            act(abc[:], pk2[:, ALOG:ALOG + 32], AF.Exp, ["pk2"], ["abc"])
            tsc(DVE, abc[:], abc[:], -1.0, None, ALU.mult, None, ["abc"], ["abc"])

            mark(2)
            act(scs, c2s, AF.Silu, ["A:c2s"], ["A:scs"])
            cp(DVE, scb, scs.unsqueeze(2).to_broadcast([128, 8, 128]), ["A:scs"], ["A:scb"])
            adaw_v = adaw_d.rearrange("(k p) c -> p k c", p=128)
            for n in range(6):
                buf = adaw[n % 2]
                key = "A:adaw%d" % (n % 2)
                dma(SP, buf, adaw_v[:, :, 512 * n:512 * (n + 1)], [], [key])
                b = n % 2
                for k in range(8):
                    mm(banks[b][:], scb[:, k, :], buf[:, k, :], k == 0, k == 7, ["A:scb", key], [bk(b)])
                tt(DVE, modbc[:, 512 * n:512 * (n + 1)], banks[b][:], pk1[:, 512 * n:512 * (n + 1)], ALU.add,
                   [bk(b)] + PK1, ["modbc"])
            stt(modbc[:, D:2 * D], modbc[:, D:2 * D], 1.0, pk1[:, 3072:4096], ALU.add, ALU.mult,
                ["modbc"] + PK1, ["modbc"])

            dbg("modbc", modbc[:], ["modbc"])
            dbg("abc", abc[:], ["abc"])
            mark(3)
            c1, c2c, c3 = _cw_consts()
            MAGIC = 12582912.0
            PI_IN = 3.1415925
            cp(DVE, posf, posi, ["A:posi"], ["A:posf"])

            def rope_table(rot, cos_t, sin_t, tag):
                half = rot // 2
                for i in range(half):
                    v = float(np.float32(500000.0) ** np.float32(-(2.0 * i) / rot))
                    memset(POOL, inv[:, i:i + 1], v, ["A:inv"])
                n = NT * half
                a3 = ang[:, 0:n].rearrange("p (j f) -> p j f", f=half)
                tt(DVE, a3, posf.unsqueeze(2).to_broadcast([128, NT, half]),
                   inv[:, 0:half].unsqueeze(1).to_broadcast([128, NT, half]), ALU.mult,
                   ["A:posf", "A:inv"], ["A:ang"])
                a = ang[:, 0:n]
                q = nn_[:, 0:n]
                r = rr_[:, 0:n]
                ab = aa_[:, 0:n]
                tsc(DVE, q, a, float(1.0 / (2.0 * np.pi)), MAGIC, ALU.mult, ALU.add, ["A:ang"], ["A:nn"])
                tsc(DVE, q, q, MAGIC, None, ALU.subtract, None, ["A:nn"], ["A:nn"])
                stt(r, q, -c1, a, ALU.mult, ALU.add, ["A:nn", "A:ang"], ["A:rr"])
                stt(r, q, -c2c, r, ALU.mult, ALU.add, ["A:nn", "A:rr"], ["A:rr"])
                stt(r, q, -c3, r, ALU.mult, ALU.add, ["A:nn", "A:rr"], ["A:rr"])
                tsc(DVE, r, r, PI_IN, -PI_IN, ALU.min, ALU.max, ["A:rr"], ["A:rr"])
                act(sin_t[:].rearrange("p j f -> p (j f)"), r, AF.Sin, ["A:rr"], [tag + "sin"])
                act(ab, r, AF.Abs, ["A:rr"], ["A:aa"])
                tsc(DVE, ab, ab, -1.0, float(np.pi / 2), ALU.mult, ALU.add, ["A:aa"], ["A:aa"])
                act(cos_t[:].rearrange("p j f -> p (j f)"), ab, AF.Sin, ["A:aa"], [tag + "cos"])

            rope_table(32, cosA, sinA, "ra")
            rope_table(16, cosI, sinI, "ri")

            dbg("cosA", cosA[:], ["racos"])
            dbg("sinA", sinA[:], ["rasin"])
            dbg("cosI", cosI[:], ["ricos"])
            mark(4)
            cast_eng = [DVE, ACT]
            for ci, (name, src, r0, c0, w) in enumerate(CHUNKS):
                b = ci % 2
                skey = "A:adaw%d" % b
                okey = "A:wbf%d" % b
                if isinstance(c0, list):
                    for pi, cc0 in enumerate(c0):
                        srcv = wsrc[src][r0:r0 + 1024, cc0:cc0 + w].rearrange("(k p) c -> p k c", p=128)
                        dma(SP, wst[b][:, :, pi * w:(pi + 1) * w], srcv, [], [skey])
                    w = w * len(c0)
                else:
                    srcv = wsrc[src][r0:r0 + 1024, c0:c0 + w].rearrange("(k p) c -> p k c", p=128)
                    dma(SP, wst[b][:, :, 0:w], srcv, [], [skey])
                stg = wst[b][:, :, 0:w]
                obf = wbf[b][:, :, 0:w]
                if src == "w_bs":
                    for k in range(8):
                        kk = (r0 // 128) + k
                        tsc(DVE, obf[:, k, :], stg[:, k, :], gs16[:, kk:kk + 1], None,
                            ALU.mult, None, [skey, "gs16"], [okey])
                else:
                    e = cast_eng[ci % 2]
                    for half in range(2):
                        cp(e, obf[:, 4 * half:4 * half + 4, :], stg[:, 4 * half:4 * half + 4, :], [skey], [okey])
                dstv = wsc_d[ci].rearrange("p (k c) -> p k c", c=512)[:, :, 0:w]
                dma(ACT, dstv, obf, [okey], ["wsc%d" % ci])

            _cdt = CHIDX["dt"]
            dma(SP, dtw[:], wsc_d[_cdt].rearrange("p (k c) -> p k c", c=512)[:, :, 0:32], ["wsc%d" % _cdt], ["dtw"])
            mark(5)
            ring_i = [0]

            def load_w(name):
                i = ring_i[0] % NRING
                ring_i[0] += 1
                ci = CHIDX[name]
                w = CHW[name]
                key = "ring%d" % i
                src = wsc_d[ci].rearrange("p (k c) -> p k c", c=512)[:, :, 0:w]
                dma(SP, ring[i][:, :, 0:w], src, ["wsc%d" % ci], [key])
                return ring[i], key

            evac_i = [0]

            def evac_eng():
                evac_i[0] += 1
                return ACT if evac_i[0] % 2 == 0 else DVE

            def proj_tok(wt, wkey, u, ncols, c_off=0):
                b = gen_bank()
                for k in range(8):
                    mm(banks[b][:, 0:ncols], hT[:, k, 128 * u:128 * (u + 1)], wt[:, k, c_off:c_off + ncols],
                       k == 0, k == 7, ["hT%d" % u, wkey], [bk(b)])
                return b

            def rope_apply(eng, dst, src, nh, hd, half, cos_t, sin_t, j, tmpa, tmpb, rd, wr, tkey, copy_eng=None):
                s3 = src.rearrange("p (h d) -> p h d", d=hd)
                d3 = dst.rearrange("p (h d) -> p h d", d=hd)
                x1 = s3[:, :, 0:half]
                x2 = s3[:, :, half:2 * half]
                cb = cos_t[:, j, :].unsqueeze(1).to_broadcast([128, nh, half])
                sbb = sin_t[:, j, :].unsqueeze(1).to_broadcast([128, nh, half])
                ta = tmpa[:, 0:nh * half].rearrange("p (h d) -> p h d", d=half)
                tb = tmpb[:, 0:nh * half].rearrange("p (h d) -> p h d", d=half)
                tt(eng, ta, x1, cb, ALU.mult, rd, [tkey + "a"])
                tt(eng, tb, x2, sbb, ALU.mult, rd, [tkey + "b"])
                tt(eng, d3[:, :, 0:half], ta, tb, ALU.subtract, [tkey + "a", tkey + "b"], wr)
                tt(eng, ta, x2, cb, ALU.mult, rd, [tkey + "a"])
                tt(eng, tb, x1, sbb, ALU.mult, rd, [tkey + "b"])
                tt(eng, d3[:, :, half:2 * half], ta, tb, ALU.add, [tkey + "a", tkey + "b"], wr)
                cp(copy_eng if copy_eng is not None else eng, d3[:, :, 2 * half:hd], s3[:, :, 2 * half:hd], rd, wr)

            out_ops = []

            for I in range(NSUP):
                mark(100 + I)
                CV.reset()
                gen_pool[:] = [0, 1, 2, 3, 4, 5, 6, 7]
                sc = CV.f32(SEQ)
                xt = [sc[:, 0:D], sc[:, D:2 * D]]
                mbw = CV.f32(SEQ // 2)
                mbw2 = CV.f32(SEQ // 2)
                htmps = [sc[:, 2 * D:3 * D], mbw[:, 0:D]]
                hbfs = [sc[:, 3 * D:3 * D + D // 2].bitcast(BF16), mbw[:, D:D + D // 2].bitcast(BF16)]
                mbs = [mbw.bitcast(BF16), mbw2.bitcast(BF16)]
                Oraw = CV.f32(8 * 130).rearrange("p (h d) -> p h d", d=130)
                qraw = [CV.f32(512), CV.f32(512)]
                qnrm = CV.f32(512)
                qbf = CV.bf16(D)
                tmpc = CV.f32(64)
                tmpd = CV.f32(64)
                ssq = CV.f32(8)
                kraws = [CV.f32(256) for _ in range(NSUB)]
                knrms = [CV.f32(256) for _ in range(NSUB)]
                kbfs = [CV.bf16(256) for _ in range(NSUB)]
                iraw = CV.f32(324 * NSUB).rearrange("p (u c) -> p u c", c=324)
                kins = [CV.f32(64) for _ in range(NSUB)]
                kjunks = [CV.f32(64) for _ in range(NSUB)]
                kibfs = [CV.bf16(64) for _ in range(NSUB)]
                qibfs = [CV.bf16(256) for _ in range(NSUB)]
                wq = CV.f32(4 * NSUB).rearrange("p (u c) -> p u c", c=4)
                tmpas = [CV.f32(64) for _ in range(NSUB)]
                tmpbs = [CV.f32(64) for _ in range(NSUB)]
                st4s = [CV.f32(32) for _ in range(NSUB)]
                st4 = st4s[0]
                qT = CV.bf16(NSUB * D).rearrange("p (u h t) -> p u h t", h=8, t=128)
                qidxT = CV.bf16(NSUB * 512).rearrange("p (u h t) -> p u h t", h=4, t=128)
                szb = CV.bf16(NSUB * D).rearrange("p (u c) -> p u c", c=D)
                relu_t = [CV.f32(512) for _ in range(2)]
                PTb = [CV.bf16(512) for _ in range(3)]
                ogb = CV.bf16(D)
                ogT = CV.bf16(D).rearrange("p (h t) -> p h t", t=128)
                bis = CV.f32(16 + 2 * NITER)
                rl = CV.f32(8)
                bmx = CV.f32(256)
                top8 = CV.f32(8)
                sqjunk = CV.bf16(128)

                wkv, wkvk = load_w("kv")
                widx, widxk = load_w("idx")

                def p12(u):
                    j = I * NSUB + u
                    us = str(u)
                    xb = xt[u]
                    xkey = "A:xt" + us
                    hk = "A:htmp" + us
                    sq = ssq[:, 4 * u:4 * u + 4]
                    s4 = st4s[u]
                    dma(SP, xb, x_d[128 * j:128 * (j + 1), :], [], [xkey])
                    act(htmps[u], xb, AF.Square, [xkey], [hk], accum=sq[:, 0:1])
                    yield
                    rstd_pool(sq[:, 2:3], sq[:, 1:2], sq[:, 0:1], 1, 1.0 / D, [hk], ["A:ssq2" + us], "A:ssq1" + us)
                    stt(htmps[u], xb, sq[:, 2:3], modbc[:, D:2 * D], ALU.mult, ALU.mult,
                        [xkey, "A:ssq2" + us, "modbc", hk], [hk])
                    tt(DVE, hbfs[u], htmps[u], modbc[:, 0:D], ALU.add, [hk, "modbc"], ["A:hbf" + us])
                    yield
                    b = gen_bank()
                    pv = banks[b][:].bitcast(BF16).rearrange("p (k t) -> p k t", t=128)
                    for k in range(8):
                        tr(pv[:, k, :], hbfs[u][:, 128 * k:128 * (k + 1)], ident_b[:], ["A:hbf" + us, "ident_b"], [bk(b)])
                    yield
                    cp(ACT, hT[:, :, 128 * u:128 * (u + 1)], pv[:, 0:8, :], [bk(b)], ["hT" + us])
                    yield
                    b = proj_tok(wkv, wkvk, u, 512)
                    yield
                    cp(ACT, Vaug[:, j, :, 0:128], banks[b][:, 256:512].rearrange("p (g d) -> p g d", d=128),
                       [bk(b)], ["V%d" % j])
                    kr = kraws[u]
                    kn = knrms[u]
                    cp(ACT, kr, banks[b][:, 0:256], [bk(b)], ["A:kraw" + us])
                    for g in range(2):
                        act(kn[:, 128 * g:128 * (g + 1)], kr[:, 128 * g:128 * (g + 1)], AF.Square,
                            ["A:kraw" + us], ["A:knrm" + us], accum=s4[:, g:g + 1])
                    yield
                    rstd_pool(s4[:, 4:6], s4[:, 2:4], s4[:, 0:2], 2, 1.0 / 128, ["A:knrm" + us], ["A:st4c" + us],
                              "A:st4b" + us)
                    yield
                    for g in range(2):
                        stt(kn[:, 128 * g:128 * (g + 1)], kr[:, 128 * g:128 * (g + 1)], s4[:, 4 + g:5 + g],
                            pk2[:, KG:KG + 128], ALU.mult, ALU.mult,
                            ["A:kraw" + us, "A:st4c" + us, "pk2", "A:knrm" + us], ["A:knrm" + us])
                    rope_apply(DVE, kbfs[u], kn, 2, 128, 16, cosA, sinA, j, tmpas[u], tmpbs[u],
                               ["A:knrm" + us, "racos", "rasin"], ["A:kbf" + us], "A:tk" + us)
                    yield
                    b2 = gen_bank()
                    pv = banks[b2][:].bitcast(BF16).rearrange("p (k t) -> p k t", t=128)
                    for g in range(2):
                        tr(pv[:, g, :], kbfs[u][:, 128 * g:128 * (g + 1)], ident_b[:], ["A:kbf" + us, "ident_b"], [bk(b2)])
                    yield
                    cp(ACT, KT[:, :, 128 * j:128 * (j + 1)], pv[:, 0:2, :], [bk(b2)], ["K%d" % j])
                    yield
                    b = proj_tok(widx, widxk, u, 324)
                    ik = "A:iraw" + us
                    yield
                    cp(ACT, iraw[:, u, :], banks[b][:, 0:324], [bk(b)], [ik])
                    kx = iraw[:, u, 256:320]
                    kin = kins[u]
                    kjunk = kjunks[u]
                    tsc(DVE, kjunk, kx, 1.0, 0.0, ALU.mult, ALU.add, [ik], ["A:kjunk" + us], accum=s4[:, 6:7])
                    tsc(DVE, s4[:, 7:8], s4[:, 6:7], 1.0 / 64, None, ALU.mult, None, ["A:kjunk" + us], ["A:bn2" + us])
                    tsc(DVE, kin, kx, s4[:, 7:8], None, ALU.subtract, None, [ik, "A:bn2" + us], ["A:kin" + us])
                    yield
                    act(kjunk, kin, AF.Square, ["A:kin" + us, "A:kjunk" + us], ["A:kjunk2" + us], accum=s4[:, 8:9])
                    rstd_pool(s4[:, 10:11], s4[:, 9:10], s4[:, 8:9], 1, 1.0 / 64, ["A:kjunk2" + us], ["A:bn4" + us],
                              "A:bn3" + us)
                    yield
                    tsc(DVE, kin, kin, s4[:, 10:11], None, ALU.mult, None, ["A:kin" + us, "A:bn4" + us], ["A:kin" + us])
                    tt(DVE, kin, kin, pk2[:, IG:IG + 64], ALU.mult, ["A:kin" + us, "pk2"], ["A:kin" + us])
                    tt(DVE, kin, kin, pk2[:, IB:IB + 64], ALU.add, ["A:kin" + us, "pk2"], ["A:kin" + us])
                    rope_apply(DVE, kibfs[u], kin, 1, 64, 8, cosI, sinI, j, tmpas[u], tmpbs[u],
                               ["A:kin" + us, "ricos", "risin"], ["A:kibf" + us], "A:tk" + us)
                    yield
                    b2 = gen_bank()
                    pv = banks[b2][:].bitcast(BF16)
                    tr(pv[0:64, 0:128], kibfs[u], ident_b[:], ["A:kibf" + us, "ident_b"], [bk(b2)])
                    yield
                    cp(ACT, kidxT[:, 128 * j:128 * (j + 1)], pv[0:64, 0:128], [bk(b2)], ["KI%d" % j])
                    yield
                    rope_apply(DVE, qibfs[u], iraw[:, u, 0:256], 4, 64, 8, cosI, sinI, j, tmpas[u], tmpbs[u],
                               [ik, "ricos", "risin"], ["A:qibf" + us], "A:tk" + us)
                    yield
                    b3 = gen_bank()
                    pv = banks[b3][:].bitcast(BF16).rearrange("p (k t) -> p k t", t=128)
                    for hh in range(4):
                        tr(pv[0:64, hh, :], qibfs[u][:, 64 * hh:64 * (hh + 1)], ident_b[:], ["A:qibf" + us, "ident_b"],
                           [bk(b3)])
                    yield
                    cp(ACT, qidxT[0:64, u, :, :], pv[0:64, 0:4, :], [bk(b3)], ["A:qidxT%d" % u])
                    tsc(DVE, wq[:, u, :], iraw[:, u, 320:324], 0.0625, None, ALU.mult, None, [ik], ["A:wq%d" % u])
                    yield

                gens = [p12(u) for u in range(NSUB)]
                while gens:
                    for gq in list(gens):
                        try:
                            next(gq)
                        except StopIteration:
                            gens.remove(gq)
                HTK = ["hT%d" % uu for uu in range(NSUB)]
                dbg("hT", hT[:], HTK, BF16)
                mark(7)

                dbg("KT", KT[:, :, 0:TS], ["K%d" % jj for jj in range(NSUB)], BF16)
                dbg("V", Vaug[:, 0:NSUB, :, :], ["V%d" % jj for jj in range(NSUB)] + ["Vones"], BF16)
                dbg("kidxT", kidxT[:, 0:TS], ["KI%d" % jj for jj in range(NSUB)], BF16)
                dbg("qidxT", qidxT[0:64], ["A:qidxT%d" % uu for uu in range(NSUB)], BF16)
                dbg("wq", wq, ["A:wq%d" % uu for uu in range(NSUB)])
                def p3a():
                    gen_pool[:] = [2, 3, 4, 5, 6, 7]
                    blocks = [(half, u) for half in range(2) for u in range(NSUB)]
                    wts = {}

                    def stage_a(half, u):
                        if half not in wts:
                            wts[half] = load_w("q%d" % half)
                        wt, wkey = wts[half]
                        j = I * NSUB + u
                        b = proj_tok(wt, wkey, u, 512)
                        qr = qraw[(half * NSUB + u) % 2]
                        qk = "A:qraw%d" % ((half * NSUB + u) % 2)
                        cp(ACT, qr, banks[b][:], [bk(b)], [qk])
                        for hh in range(4):
                            act(sqjunk, qr[:, 128 * hh:128 * (hh + 1)], AF.Square,
                                [qk], ["A:sqjunk"], accum=st4[:, 16 + hh:17 + hh])
                        rstd_pool(st4[:, 24:28], st4[:, 20:24], st4[:, 16:20], 4, 1.0 / 128, ["A:sqjunk"], ["A:st4qc"],
                                  "A:st4qb")
                        for hh in range(4):
                            tsc(POOL, qnrm[:, 128 * hh:128 * (hh + 1)], qr[:, 128 * hh:128 * (hh + 1)],
                                st4[:, 24 + hh:25 + hh], 1.0, ALU.mult, ALU.mult, [qk, "A:st4qc", "A:qnrm"], ["A:qnrm"])
                        q3 = qnrm.rearrange("p (h d) -> p h d", d=128)
                        tt(POOL, q3, q3, pk2[:, QG:QG + 128].unsqueeze(1).to_broadcast([128, 4, 128]), ALU.mult,
                           ["A:qnrm", "pk2"], ["A:qnrm"])
                        qb = "A:qbf%d_%d" % (u, half)
                        rope_apply(POOL, qbf[:, 512 * half:512 * (half + 1)], qnrm, 4, 128, 16, cosA, sinA, j,
                                   tmpc, tmpd, ["A:qnrm", "racos", "rasin"], [qb], "A:tp")

                    def stage_b(half, u):
                        qb = "A:qbf%d_%d" % (u, half)
                        b2 = gen_bank()
                        pv = banks[b2][:].bitcast(BF16).rearrange("p (k t) -> p k t", t=128)
                        for hh in range(4):
                            tr(pv[:, hh, :], qbf[:, 512 * half + 128 * hh:512 * half + 128 * (hh + 1)], ident_b[:],
                               [qb, "ident_b"], [bk(b2)])
                        cp(ACT, qT[:, u, 4 * half:4 * half + 4, :], pv[:, 0:4, :], [bk(b2)], ["A:qT%d_%d" % (u, half)])

                    for i, (half, u) in enumerate(blocks):
                        stage_a(half, u)
                        yield
                        if i > 0:
                            stage_b(*blocks[i - 1])
                            yield
                    stage_b(*blocks[-1])
                    yield
                    for half in range(2):
                        wt, wkey = load_w("z%d" % half)
                        for u in range(NSUB):
                            b = proj_tok(wt, wkey, u, 512)
                            act(szb[:, u, 512 * half:512 * (half + 1)], banks[b][:], AF.Silu, [bk(b)],
                                ["A:sz%d_%d" % (u, half)])


                lo, hi, rng, mid, cnt, sgn, thr, sA, tot = [bis[:, i:i + 1] for i in range(9)]
                steps = bis[:, 16:16 + NITER]
                pw = bis[:, 16 + NITER:16 + 2 * NITER]
                for k in range(NITER):
                    memset(POOL, pw[:, k:k + 1], float(2.0 ** -(k + 1)), ["A:pw"])

                def idx_part(u):
                    j = I * NSUB + u
                    nk = j + 1
                    L = 128 * nk
                    for g0 in range(0, nk, 4):
                        g1 = min(nk, g0 + 4)
                        wdt = 128 * (g1 - g0)
                        rdk = ["KI%d" % kb for kb in range(g0, g1)]
                        for hh in range(4):
                            b = gen_bank()
                            mm(banks[b][:, 0:wdt], qidxT[0:64, u, hh, :], kidxT[:, 128 * g0:128 * g1], True, True,
                               ["A:qidxT%d" % u] + rdk, [bk(b)])
                            rt = relu_t[hh % 2]
                            rk = "A:relu%d" % (hh % 2)
                            act(rt[:, 0:wdt], banks[b][:, 0:wdt], AF.Relu, [bk(b)], [rk])
                            dst = sc[:, 128 * g0:128 * g1]
                            if hh == 0:
                                tsc(DVE, dst, rt[:, 0:wdt], wq[:, u, 0:1], None, ALU.mult, None,
                                    [rk, "A:wq%d" % u], ["A:sc"])
                            else:
                                stt(dst, rt[:, 0:wdt], wq[:, u, hh:hh + 1], dst, ALU.mult, ALU.add,
                                    [rk, "A:wq%d" % u, "A:sc"], ["A:sc"])
                    if j >= 2:
                        bs_ = j // 2
                        if bs_ > 1:
                            src3 = sc[:, 0:256 * bs_].rearrange("p (b s) -> p b s", s=bs_)
                            S.add(DVE, lambda e, src3=src3: e.tensor_reduce(bmx, src3, mybir.AxisListType.X, ALU.max),
                                  rd=["A:sc"], wr=["A:bmx"])
                            S.add(DVE, lambda e: e.tensor_reduce(lo, bmx, mybir.AxisListType.X, ALU.min),
                                  rd=["A:bmx"], wr=["A:lo"])
                        else:
                            S.add(DVE, lambda e: e.tensor_reduce(lo, sc[:, 0:256], mybir.AxisListType.X, ALU.min),
                                  rd=["A:sc"], wr=["A:lo"])
                    dg = sc[:, 128 * j:128 * (j + 1)]
                    S.add(POOL, lambda e, dg=dg: e.affine_select(dg, dg, [[-1, 128]], ALU.is_ge, BIGNEG,
                                                                base=0, channel_multiplier=1),
                          rd=["A:sc", "A:lo"], wr=["A:sc"])
                    if j >= 2:
                        S.add(DVE, lambda e, L=L: e.max(out=top8, in_=sc[:, 0:L]), rd=["A:sc"], wr=["A:top8"])
                        S.add(DVE, lambda e: e.tensor_reduce(hi, top8, mybir.AxisListType.X, ALU.min),
                              rd=["A:top8"], wr=["A:hi"])
                    if j >= 2:
                        tsc(DVE, lo, lo, 1.0e-6, None, ALU.subtract, None, ["A:lo"], ["A:lo"])
                        tt(DVE, rng, hi, lo, ALU.subtract, ["A:hi", "A:lo"], ["A:rng"])
                        tsc(DVE, steps, pw, rng, None, ALU.mult, None, ["A:pw", "A:rng"], ["A:steps"])
                        tt(DVE, mid, lo, steps[:, 0:1], ALU.add, ["A:lo", "A:steps"], ["A:mid"])
                    else:
                        memset(DVE, thr, -1.0e29, ["A:thr"])

                def bisect_gen(u, dve_frac):
                    j = I * NSUB + u
                    nk = j + 1
                    L = 128 * nk
                    if j < 2:
                        return
                    junk = mbs[j % 2]
                    kD = "A:mbD%d" % (j % 2)
                    kA = "A:mbA%d" % (j % 2)
                    nkD = nk if (nk < 6 or dve_frac >= 1.0) else max(1, int(round(dve_frac * nk)))
                    La = 128 * nkD
                    nA = L - La
                    for k in range(NITER):
                        tsc(DVE, junk[:, 0:La], sc[:, 0:La], mid, 0.0, ALU.is_gt, ALU.add,
                            ["A:sc", "A:mid"], [kD, "A:cnt"], accum=cnt)
                        if nA > 0:
                            act(junk[:, La:L], sc[:, La:L], AF.Sign, ["A:sc", "A:mid"], [kA, "A:sA"],
                                bias=mid, scale=-1.0, accum=sA)
                            stt(tot, sA, -0.5, cnt, ALU.mult, ALU.add, ["A:sA", "A:cnt"], ["A:tot"])
                            tsc(DVE, sgn, tot, float(TOPK) - 0.5 * nA, -0.5, ALU.is_ge, ALU.add, ["A:tot"], ["A:sgn"])
                        else:
                            tsc(DVE, sgn, cnt, float(TOPK), -0.5, ALU.is_ge, ALU.add, ["A:cnt"], ["A:sgn"])
                        if k < NITER - 1:
                            stt(mid, sgn, steps[:, k:k + 1], mid, ALU.mult, ALU.add,
                                ["A:sgn", "A:steps", "A:mid"], ["A:mid"])
                        else:
                            tsc(DVE, sgn, sgn, 0.5, None, ALU.subtract, None, ["A:sgn"], ["A:sgn"])
                            stt(thr, sgn, steps[:, k:k + 1], mid, ALU.mult, ALU.add,
                                ["A:sgn", "A:steps", "A:mid"], ["A:thr"])
                        yield

                def do_mask(u):
                    j = I * NSUB + u
                    L = 128 * (j + 1)
                    tsc(DVE, mbs[j % 2][:, 0:L], sc[:, 0:L], thr, NEG, ALU.is_le, ALU.mult, ["A:sc", "A:thr"],
                        ["A:mb%d" % (j % 2), "A:mbD%d" % (j % 2), "A:mbA%d" % (j % 2)])
                    if j == (debug.get("j", 1) if isinstance(debug, dict) else 1):
                        dbg("sc", sc[:, 0:L], ["A:sc"])
                        dbg("bis", bis, ["A:thr"])
                        dbg("mb", mbs[j % 2][:, 0:L], ["A:mb%d" % (j % 2)], BF16)

                def att_gen(u):
                    j = I * NSUB + u
                    nk = j + 1
                    mbj = mbs[j % 2]
                    mkey = "A:mb%d" % (j % 2)
                    scale = 128.0 ** -0.5
                    unitsl = [(g, kb) for g in range(2) for kb in range(nk)]

                    def s_stage(i):
                        g, kb = unitsl[i]
                        sb_ = 2 + (i % 2)
                        mm(banks[sb_][:], KT[:, g, 128 * kb:128 * (kb + 1)],
                           qT[:, u, 4 * g:4 * g + 4, :].rearrange("p h t -> p (h t)"), True, False,
                           ["K%d" % kb, "A:qT%d_%d" % (u, g)], [bk(sb_)])
                        mm(banks[sb_][:], mbj[:, 128 * kb:128 * (kb + 1)], I4_b[:], False, True,
                           [mkey, "A:mbD%d" % (j % 2), "A:mbA%d" % (j % 2), "I4_b"], [bk(sb_)])

                    s_stage(0)
                    for i, (g, kb) in enumerate(unitsl):
                        if i + 1 < len(unitsl):
                            s_stage(i + 1)
                        sb_ = 2 + (i % 2)
                        pt = PTb[i % 3]
                        pk = "A:PT%d" % (i % 3)
                        act(pt, banks[sb_][:], AF.Exp, [bk(sb_)], [pk], scale=scale)
                        for hh in range(4):
                            ob = 4 + hh
                            mm(banks[ob][:, 0:130], pt[:, 128 * hh:128 * (hh + 1)], Vaug[:, kb, g, :],
                               kb == 0, kb == nk - 1, [pk, "V%d" % kb, "Vones"], [bk(ob)])
                        if kb == nk - 1:
                            for hh in range(4):
                                h = 4 * g + hh
                                cp(ACT, Oraw[:, h, :], banks[4 + hh][:, 0:130], [bk(4 + hh)], ["A:Oraw%d" % h])
                        yield

                def att_fin(u):
                    for h in range(8):
                        recip(rl[:, h:h + 1], Oraw[:, h, 128:129], ["A:Oraw%d" % h], ["A:rl%d" % h])
                        stt(ogb[:, 128 * h:128 * (h + 1)], Oraw[:, h, 0:128], rl[:, h:h + 1],
                            szb[:, u, 128 * h:128 * (h + 1)], ALU.mult, ALU.mult,
                            ["A:Oraw%d" % h, "A:rl%d" % h, "A:sz%d_%d" % (u, h // 4)], ["A:ogb"])
                    b = gen_bank()
                    pv = banks[b][:].bitcast(BF16).rearrange("p (k t) -> p k t", t=128)
                    for h in range(8):
                        tr(pv[:, h, :], ogb[:, 128 * h:128 * (h + 1)], ident_b[:], ["A:ogb", "ident_b"], [bk(b)])
                    cp(ACT, ogT, pv[:, 0:8, :], [bk(b)], ["A:ogT"])
                    for c in range(2):
                        wt, wkey = wba[c]
                        b = gen_bank()
                        for h in range(8):
                            mm(banks[b][:], ogT[:, h, :], wt[:, h, :], h == 0, h == 7, ["A:ogT", wkey], [bk(b)])
                        cp(ACT, yatt[:, u, 512 * c:512 * (c + 1)], banks[b][:], [bk(b)], ["yatt"])

                gen_pool[:] = [0, 1]
                idx_part(0)
                ga = bisect_gen(0, BIS_DVE_FRAC)
                gb = p3a()
                alive = [ga, gb]
                reps = [2, 1]
                while any(x is not None for x in alive):
                    for ii in range(2):
                        for _ in range(reps[ii]):
                            if alive[ii] is None:
                                break
                            try:
                                next(alive[ii])
                            except StopIteration:
                                alive[ii] = None
                do_mask(0)
                dbg("qT", qT, ["A:qT%d_%d" % (uu, hf) for uu in range(NSUB) for hf in range(2)], BF16)
                dbg("szb", szb, ["A:sz%d_%d" % (uu, hf) for uu in range(NSUB) for hf in range(2)], BF16)
                mark(8)
                wba = [load_w("ba0"), load_w("ba1")]
                gen_pool[:] = [0, 1]
                for u in range(NSUB):
                    j = I * NSUB + u
                    mark(9)
                    if u + 1 < NSUB:
                        idx_part(u + 1)
                        ga = bisect_gen(u + 1, BIS1_DVE_FRAC)
                        gb = att_gen(u)
                        alive = [ga, gb]
                        reps = [1, max(1, int(round(2.0 * (j + 1) / NITER)))]
                        while any(x is not None for x in alive):
                            for ii in range(2):
                                for _ in range(reps[ii]):
                                    if alive[ii] is None:
                                        break
                                    try:
                                        next(alive[ii])
                                    except StopIteration:
                                        alive[ii] = None
                        do_mask(u + 1)
                    else:
                        for _ in att_gen(u):
                            pass
                    att_fin(u)

                dbg("yatt", yatt[:], ["yatt"])
                mark(10)
                CV.reset()
                gen_pool[:] = [0, 1]
                dtt = CV.f32(NSUB * 32).rearrange("p (u h) -> p u h", h=32)
                adt = CV.f32(NSUB * 32).rearrange("p (u h) -> p u h", h=32)
                eat = CV.f32(NSUB * 32).rearrange("p (u h) -> p u h", h=32)
                cdb = CV.f32(NSUB * 32).rearrange("p (u h) -> p u h", h=32)
                sp1 = CV.f32(32)
                sp2 = CV.f32(32)
                sp3 = CV.f32(32)
                uraw = [CV.f32(TS + 3) for _ in range(2)]
                cacc = [CV.f32(6 * TS).rearrange("p (c t) -> p c t", t=TS) for _ in range(2)]
                BTb = [CV.bf16(TS) for _ in range(2)]
                CTb = [CV.bf16(TS) for _ in range(2)]
                szs = [CV.f32(NSUB * 512).rearrange("p (u c) -> p u c", c=512) for _ in range(2)]
                Rm = [CV.f32(1024).rearrange("p (h l) -> p h l", l=128) for _ in range(NSUB)]
                decT = [CV.f32(1024).rearrange("p (h l) -> p h l", l=128) for _ in range(NSUB)]
                CBm = [CV.f32(128) for _ in range(NSUB)]
                MTb = [CV.bf16(1024).rearrange("p (h l) -> p h l", l=128) for _ in range(NSUB)]
                xstok = [CV.f32(512) for _ in range(NSUB)]
                xcb = [CV.bf16(512) for _ in range(NSUB)]
                xcdb = [CV.bf16(512) for _ in range(NSUB)]
                dtds = [CV.f32(8) for _ in range(NSUB)]
                Btok = [CV.bf16(128) for _ in range(NSUB)]
                y1 = CV.f32(512)
                y2 = CV.f32(512)
                ssn = CV.f32(4)
                ynb = CV.bf16(512)
                stmp = CV.f32(512)

                wt, wkey = dtw, "dtw"
                for u in range(NSUB):
                    b = proj_tok(wt, wkey, u, 32)
                    tt(DVE, sp1, banks[b][:, 0:32], pk2[:, DTB:DTB + 32], ALU.add, [bk(b), "pk2"], ["A:sp1"])
                    act(sp2, sp1, AF.Abs, ["A:sp1"], ["A:sp2"])
                    act(sp2, sp2, AF.Exp, ["A:sp2"], ["A:sp2"], scale=-1.0)
                    act(sp3, sp2, AF.Ln, ["A:sp2"], ["A:sp3"], bias=1.0)
                    stt(dtt[:, u, :], sp1, 0.0, sp3, ALU.max, ALU.add, ["A:sp1", "A:sp3"], ["A:dtt%d" % u])
                    tt(DVE, adt[:, u, :], dtt[:, u, :], abc[:], ALU.mult, ["A:dtt%d" % u, "abc"], ["A:adt%d" % u])
                    b = gen_bank()
                    mm(banks[b][:, 0:32], U_f[:], adt[:, u, :], True, True, ["U_f", "A:adt%d" % u], [bk(b)])
                    mm(banks[b][:, 32:64], ones_f[:], adt[:, u, :], True, True, ["ones_f", "A:adt%d" % u], [bk(b)])
                    act(eat[:, u, :], banks[b][:, 0:32], AF.Exp, [bk(b)], ["A:eat%d" % u])
                    act(cdb[:, u, :], banks[b][:, 32:64], AF.Exp, [bk(b)], ["A:cdb%d" % u])

                def Gstage(g):
                    st_ = g % 2
                    ss_ = str(st_)
                    wx, wxk = load_w("xs%d" % g)
                    wbc_, wbck = load_w("bc%d" % g)
                    srcs = [(wx, wxk, 128 * c) for c in range(4)] + [(wbc_, wbck, 0), (wbc_, wbck, 128)]
                    chidx = [4 * g + c for c in range(4)] + [16 + g, 20 + g]
                    for ci6, (wt_, wk_, coff) in enumerate(srcs):
                        b = gen_bank()
                        for k in range(8):
                            mm(banks[b][:, 0:TS], wt_[:, k, coff:coff + 128], hT[:, k, :], k == 0, k == 7,
                               [wk_] + HTK, [bk(b)])
                        ur = uraw[ci6 % 2]
                        ck = "A:uraw%d" % (ci6 % 2)
                        cc = chidx[ci6]
                        cp(POOL, ur[:, 0:3], ctail[:, cc, :], ["ctail%d" % cc], [ck])
                        cp(ACT, ur[:, 3:3 + TS], banks[b][:, 0:TS], [bk(b)], [ck])
                        cp(POOL, ctail[:, cc, :], ur[:, TS:TS + 3], [ck], ["ctail%d" % cc])
                        ca = cacc[st_][:, ci6, :]
                        ak = "A:cacc%d_%s" % (ci6, ss_)
                        act(ca, ur[:, 3:3 + TS], AF.Identity, [ck, "convp"], [ak],
                            bias=convp[:, cc, 4:5], scale=convp[:, cc, 3:4])
                        yield
                        for kk in range(3):
                            stt(ca, ur[:, kk:kk + TS], convp[:, cc, kk:kk + 1], ca,
                                ALU.mult, ALU.add, [ck, "convp", ak], [ak])
                        yield
                    for ci6 in range(6):
                        ca = cacc[st_][:, ci6, :]
                        ak = "A:cacc%d_%s" % (ci6, ss_)
                        if ci6 < 4:
                            act(ca, ca, AF.Silu, [ak], [ak])
                        elif ci6 == 4:
                            act(BTb[st_], ca, AF.Silu, [ak], ["A:BTb" + ss_])
                        else:
                            act(CTb[st_], ca, AF.Silu, [ak], ["A:CTb" + ss_])
                        if ci6 == 2:
                            yield
                    yield
                    wz, wzk = load_w("zs%d" % g)
                    for u in range(NSUB):
                        b = proj_tok(wz, wzk, u, 512)
                        act(szs[st_][:, u, :], banks[b][:], AF.Silu, [bk(b)], ["A:szs%d_%s" % (u, ss_)])
                        yield

                def units(g):
                    st_ = g % 2
                    ss_ = str(st_)
                    xsT = cacc[st_]
                    hs = slice(8 * g, 8 * g + 8)

                    def front(u):
                        tsl = slice(128 * u, 128 * (u + 1))
                        us = str(u)
                        tt(POOL, Rm[u], U_f[:].unsqueeze(1).to_broadcast([128, 8, 128]),
                           adt[:, u, hs].unsqueeze(2).to_broadcast([128, 8, 128]), ALU.mult,
                           ["U_f", "A:adt%d" % u], ["A:Rm" + us])
                        yield
                        for hf in range(2):
                            mm(banks[2 + hf][:], G_f[:], Rm[u][:, 4 * hf:4 * hf + 4, :].rearrange("p h l -> p (h l)"),
                               True, True, ["G_f", "A:Rm" + us], [bk(2 + hf)])
                            act(decT[u][:, 4 * hf:4 * hf + 4, :].rearrange("p h l -> p (h l)"), banks[2 + hf][:], AF.Exp,
                                [bk(2 + hf)], ["A:decT%d_%s" % (hf, us)])
                        yield
                        mm(banks[7][:, 0:128], BTb[st_][:, tsl], CTb[st_][:, tsl], True, True,
                           ["A:BTb" + ss_, "A:CTb" + ss_], [bk(7)])
                        tt(DVE, CBm[u], banks[7][:, 0:128], U_f[:], ALU.mult, [bk(7), "U_f"], ["A:CBm" + us])
                        tt(DVE, MTb[u], decT[u], CBm[u].unsqueeze(1).to_broadcast([128, 8, 128]), ALU.mult,
                           ["A:decT0_" + us, "A:decT1_" + us, "A:CBm" + us], ["A:MTb" + us])
                        yield
                        b = gen_bank()
                        for c in range(4):
                            tr(banks[b][:, 128 * c:128 * (c + 1)], xsT[:, c, tsl], ident_f[:],
                               ["A:cacc%d_%s" % (c, ss_), "ident_f"], [bk(b)])
                        yield
                        cp(ACT, xstok[u], banks[b][:], [bk(b)], ["A:xstok" + us])
                        x3 = xstok[u].rearrange("p (h d) -> p h d", d=64)
                        tt(POOL, xcb[u].rearrange("p (h d) -> p h d", d=64), x3,
                           dtt[:, u, hs].unsqueeze(2).to_broadcast([128, 8, 64]), ALU.mult,
                           ["A:xstok" + us, "A:dtt%d" % u], ["A:xcb" + us])
                        tt(DVE, dtds[u], decT[u][:, :, 127], dtt[:, u, hs], ALU.mult,
                           ["A:decT0_" + us, "A:decT1_" + us, "A:dtt%d" % u], ["A:dtds" + us])
                        tt(POOL, xcdb[u].rearrange("p (h d) -> p h d", d=64), x3,
                           dtds[u].unsqueeze(2).to_broadcast([128, 8, 64]), ALU.mult,
                           ["A:xstok" + us, "A:dtds" + us], ["A:xcdb" + us])
                        yield
                        b = gen_bank()
                        pvb = banks[b][:].bitcast(BF16)
                        tr(pvb[:, 0:128], BTb[st_][:, tsl], ident_b[:], ["A:BTb" + ss_, "ident_b"], [bk(b)])
                        yield
                        cp(ACT, Btok[u], pvb[:, 0:128], [bk(b)], ["A:Btok" + us])
                        yield

                    def back(u):
                        tsl = slice(128 * u, 128 * (u + 1))
                        us = str(u)
                        x3 = xstok[u].rearrange("p (h d) -> p h d", d=64)
                        bY, bO, bS = [(4, 5, 6), (2, 3, 7)][u % 2]
                        for hh in range(8):
                            mm(banks[bY][:, 64 * hh:64 * (hh + 1)], MTb[u][:, hh, :], xcb[u][:, 64 * hh:64 * (hh + 1)],
                               True, True, ["A:MTb" + us, "A:xcb" + us], [bk(bY)])
                        mm(banks[bO][:], CTb[st_][:, tsl], Stb[:, g, :], True, True, ["A:CTb" + ss_, "Stb%d" % g], [bk(bO)])
                        mm(banks[bS][:], Btok[u], xcdb[u], True, True, ["A:Btok" + us, "A:xcdb" + us], [bk(bS)])
                        tt(POOL, y2.rearrange("p (h d) -> p h d", d=64), x3,
                           pk2[:, DSK + 8 * g:DSK + 8 * g + 8].unsqueeze(2).to_broadcast([128, 8, 64]), ALU.mult,
                           ["A:xstok" + us, "pk2"], ["A:y2"])
                        tt(DVE, y1.rearrange("p (h d) -> p h d", d=64), banks[bO][:].rearrange("p (h d) -> p h d", d=64),
                           eat[:, u, hs].unsqueeze(2).to_broadcast([128, 8, 64]), ALU.mult,
                           [bk(bO), "A:eat%d" % u], ["A:y1"])
                        tt(DVE, y1, y1, banks[bY][:], ALU.add, ["A:y1", bk(bY)], ["A:y1"])
                        tt(DVE, y1, y1, y2, ALU.add, ["A:y1", "A:y2"], ["A:y1"])
                        tt(DVE, y1, y1, szs[st_][:, u, :], ALU.mult, ["A:y1", "A:szs%d_%s" % (u, ss_)], ["A:y1"])
                        yield
                        act(y2, y1, AF.Square, ["A:y1", "A:y2"], ["A:y2"], accum=ssn[:, 0:1])
                        rstd_pool(ssn[:, 2:3], ssn[:, 1:2], ssn[:, 0:1], 1, 1.0 / 512, ["A:y2"], ["A:ssn2"], "A:ssn1")
                        yield
                        tsc(DVE, ynb, y1, ssn[:, 2:3], None, ALU.mult, None, ["A:y1", "A:ssn2"], ["A:ynb"])
                        tt(DVE, stmp.rearrange("p (h d) -> p h d", d=64), St[:, g, :].rearrange("p (h d) -> p h d", d=64),
                           cdb[:, u, hs].unsqueeze(2).to_broadcast([128, 8, 64]), ALU.mult,
                           ["St%d" % g, "A:cdb%d" % u], ["A:stmp"])
                        tt(DVE, St[:, g, :], stmp, banks[bS][:], ALU.add, ["A:stmp", bk(bS)], ["St%d" % g])
                        cp(ACT, Stb[:, g, :], St[:, g, :], ["St%d" % g], ["Stb%d" % g])
                        b = gen_bank()
                        pv = banks[b][:].bitcast(BF16).rearrange("p (k t) -> p k t", t=128)
                        for c in range(4):
                            tr(pv[:, c, :], ynb[:, 128 * c:128 * (c + 1)], ident_b[:], ["A:ynb", "ident_b"], [bk(b)])
                        yield
                        cp(ACT, ynT[:, 4 * g:4 * g + 4, tsl], pv[:, 0:4, :], [bk(b)], ["ynT"])
                        yield

                    for u in range(NSUB):
                        yield from front(u)
                    for u in range(NSUB):
                        yield from back(u)

                def interleave(a, b):
                    gens = [x for x in (a, b) if x is not None]
                    while gens:
                        for x in list(gens):
                            try:
                                next(x)
                            except StopIteration:
                                gens.remove(x)

                GTOP = ARENA_F - 2 * NSUB * D
                assert CV.off <= GTOP, CV.off
                sga = arena[:, GTOP:GTOP + NSUB * D].rearrange("p (u c) -> p u c", c=D)
                sgs = arena[:, GTOP + NSUB * D:GTOP + 2 * NSUB * D].rearrange("p (u c) -> p u c", c=D)

                def gates_gen():
                    for c in range(2):
                        wt, wkey = load_w("ga%d" % c)
                        for u in range(NSUB):
                            b = proj_tok(wt, wkey, u, 512)
                            dst = sga[:, u, 512 * c:512 * (c + 1)]
                            gk = "G:sga%d_%d" % (u, c)
                            act(dst, banks[b][:], AF.Sigmoid, [bk(b)], [gk, "A:gguard"])
                            tt(DVE, dst, dst, yatt[:, u, 512 * c:512 * (c + 1)], ALU.mult, [gk, "yatt"], [gk])
                            yield
                    for c in range(2):
                        wt, wkey = load_w("gs%d" % c)
                        for u in range(NSUB):
                            b = proj_tok(wt, wkey, u, 512)
                            gk = "G:sgs%d_%d" % (u, c)
                            act(sgs[:, u, 512 * c:512 * (c + 1)], banks[b][:], AF.Sigmoid, [bk(b)], [gk, "A:gguard"])
                            yield

                def step(gen_):
                    try:
                        next(gen_)
                        return True
                    except StopIteration:
                        return False

                interleave(Gstage(0), None)
                for g in range(3):
                    interleave(units(g), Gstage(g + 1))
                ug = units(3)
                gg = gates_gen()
                ualive = galive = True
                ustep = 0
                while ualive or galive:
                    if ualive:
                        ualive = step(ug)
                        ustep += 1
                    if galive and (not ualive or 3 <= ustep <= 8 or ustep >= 11):
                        galive = step(gg)

                dbg("ynT", ynT[:], ["ynT"], BF16)
                dbg("St", St[:], ["St%d" % gg for gg in range(4)])
                mark(11)
                CV.reset()
                gen_pool[:] = [0, 1, 2, 3, 4, 5, 6, 7]
                m2s = [CV.f32(512) for _ in range(2)]
                mbf = CV.bf16(NSUB * D).rearrange("p (u c) -> p u c", c=D)
                mT = CV.bf16(NSUB * D).rearrange("p (u k t) -> p u k t", k=8, t=128)
                xr = [CV.f32(D) for _ in range(2)]
                ot = [CV.f32(D) for _ in range(2)]
                assert NSUB <= 2 and CV.off <= GTOP
                for u in range(NSUB):
                    j = I * NSUB + u
                    dma(ACT, xr[u % 2], x_d[128 * j:128 * (j + 1), :], [], ["A:xr%d" % (u % 2)])

                mi = 0
                for c in range(2):
                    w0, w0k = load_w("bs0%d" % c)
                    w1, w1k = load_w("bs1%d" % c)
                    for u in range(NSUB):
                        m2 = m2s[mi % 2]
                        m2k = "A:m2_%d" % (mi % 2)
                        mi += 1
                        b2 = gen_bank()
                        for kc in range(16):
                            wt_ = w0 if kc < 8 else w1
                            wk_ = w0k if kc < 8 else w1k
                            mm(banks[b2][:], ynT[:, kc, 128 * u:128 * (u + 1)], wt_[:, kc % 8, :], kc == 0, kc == 15,
                               ["ynT", wk_], [bk(b2)])
                        tt(DVE, m2, sgs[:, u, 512 * c:512 * (c + 1)], banks[b2][:], ALU.mult,
                           ["G:sgs%d_%d" % (u, c), bk(b2)], [m2k])
                        tt(DVE, mbf[:, u, 512 * c:512 * (c + 1)], m2, sga[:, u, 512 * c:512 * (c + 1)], ALU.add,
                           [m2k, "G:sga%d_%d" % (u, c)], ["A:mbf%d" % u])
                for u in range(NSUB):
                    b = gen_bank()
                    pv = banks[b][:].bitcast(BF16).rearrange("p (k t) -> p k t", t=128)
                    for k in range(8):
                        tr(pv[:, k, :], mbf[:, u, 128 * k:128 * (k + 1)], ident_b[:], ["A:mbf%d" % u, "ident_b"], [bk(b)])
                    cp(ACT, mT[:, u, :, :], pv[:, 0:8, :], [bk(b)], ["A:mT%d" % u])
                wo = [load_w("wo0"), load_w("wo1")]
                for u in range(NSUB):
                    j = I * NSUB + u
                    xk = "A:xr%d" % (u % 2)
                    okk = "A:ot%d" % (u % 2)
                    for c in range(2):
                        wt, wkey = wo[c]
                        b = gen_bank()
                        for k in range(8):
                            mm(banks[b][:], mT[:, u, k, :], wt[:, k, :], k == 0, k == 7, ["A:mT%d" % u, wkey], [bk(b)])
                        tt(DVE, ot[u % 2][:, 512 * c:512 * (c + 1)], banks[b][:], modbc[:, 2 * D + 512 * c:2 * D + 512 * (c + 1)],
                           ALU.mult, [bk(b), "modbc"], [okk])
                    tt(DVE, ot[u % 2], ot[u % 2], xr[u % 2], ALU.add, [okk, xk], [okk])
                    out_ops.append(dma(ACT, out_d[128 * j:128 * (j + 1), :], ot[u % 2], [okk], ["out%d" % j]))


        except _Stop:
            pass

        S.add(SP, None, rd=["out%d" % j for j in range(NT)] + [k for k in S.reg if k.startswith("dbg_")], wr=[])

        with nc.Block() as block:
            S.emit_all(nc, block, esem, dsem)
    return nc


def _prep_inputs(inputs):
    f = lambda a: np.ascontiguousarray(np.asarray(a), dtype=np.float32)
    x = f(inputs["x"])
    c = f(inputs["c"])
    pos = np.ascontiguousarray(np.asarray(inputs["positions"]), dtype=np.int32)
    ada_w = f(inputs["ada_w"])[0]
    pk1 = np.concatenate([f(inputs["ada_b"])[0], f(inputs["norm_g"])[0]])[None, :]
    pk2 = np.zeros((1, 512), np.float32)
    parts = [("q_norm_g", 0), ("k_norm_g", 128), ("idx_k_ln_g", 256), ("idx_k_ln_b", 320), ("dt_bias", 384),
             ("a_log", 416), ("d_skip", 448)]
    for name, off in parts:
        v = f(inputs[name])[0]
        pk2[0, off:off + v.shape[0]] = v
    conv_w = f(inputs["conv_w"])[0]
    conv_b = f(inputs["conv_b"])[0]
    convp = np.concatenate([conv_w.T, conv_b[:, None]], axis=1)
    convp = np.ascontiguousarray(convp.reshape(24, 128, 5).transpose(1, 0, 2))
    gs16 = np.ascontiguousarray(f(inputs["ssm_norm_g"])[0].reshape(16, 128).T)
    shared = {
        "ada_w": ada_w, "pk1": np.ascontiguousarray(pk1), "pk2": pk2, "convp": convp, "gs16": gs16,
        "w_in": f(inputs["w_in"])[0], "w_ba": f(inputs["w_branch_att"])[0],
        "w_bs": f(inputs["w_branch_ssm"])[0], "w_out": f(inputs["w_out"])[0],
    }
    maps = []
    for b in range(8):
        m = dict(shared)
        m["x"] = x[b]
        m["c2"] = np.ascontiguousarray(c[b].reshape(8, 128).T)
        m["pos2"] = np.ascontiguousarray(pos[b].reshape(NT, 128).T)
        maps.append(m)
    return maps


def kernel(**inputs):
    maps = _prep_inputs(inputs)
    nc = build_nc()
    res = run_bass_kernel_spmd(nc, maps, core_ids=list(range(8)))
    out = np.stack([np.asarray(r["out"], dtype=np.float32) for r in res.results], axis=0)
    return out
```
